# Optimizing a Trainium2 kernel written in Bass

```python
import numpy as np
import jax
import jax.numpy as jnp
from jax import lax

D_MODEL = 2048
BATCH = 8
SEQ = 4096
DEPTH = 4

GRID_W = 64
CTX_LEN = 256
N_MIXERS = 2
N_LAYERS_A = (DEPTH + N_MIXERS - 1) // N_MIXERS
N_LAYERS_B = DEPTH // N_MIXERS
N_MOD = 6
D_FF = 4 * D_MODEL
RMS_EPS = 1e-6

ML_HEADS = 8
ML_DV = D_MODEL // ML_HEADS
ML_DQK = ML_DV // 2
ML_QK_W = ML_HEADS * ML_DQK
ML_V_W = ML_HEADS * ML_DV
ML_IN_W = 2 * ML_QK_W + 2 * ML_V_W + 4 * ML_HEADS
ML_CHUNK = 64
GATE_CAP = 15.0
FGATE_BIAS_LO = 3.0
FGATE_BIAS_HI = 6.0
ROPE_BASE = 10000.0

NA_HEADS = 16
NA_HEAD_DIM = D_MODEL // NA_HEADS
NA_KH = 8
NA_KW = 16

kernel_name = 'hybrid_mlstm_natten_dit'


def rmsnorm(x, g):
    xf = x.astype(jnp.float32)
    xf = xf * lax.rsqrt(jnp.mean(xf * xf, axis=-1, keepdims=True) + RMS_EPS)
    return xf.astype(x.dtype) * g


def axial_rope(x, row, col):
    d = x.shape[-1]
    half = d // 2
    nf = half // 2
    inv = ROPE_BASE ** (-jnp.arange(nf, dtype=jnp.float32) / nf)

    def rot(xa, pos):
        ang = pos.astype(jnp.float32)[:, None] * inv
        cos = jnp.cos(ang)[None, :, None, :]
        sin = jnp.sin(ang)[None, :, None, :]
        x1 = xa[..., :nf].astype(jnp.float32)
        x2 = xa[..., nf:].astype(jnp.float32)
        return jnp.concatenate([x1 * cos - x2 * sin, x1 * sin + x2 * cos], axis=-1)

    out = jnp.concatenate([rot(x[..., :half], row), rot(x[..., half:], col)], axis=-1)
    return out.astype(x.dtype)


def _to_chunks(a):
    b, t = a.shape[:2]
    a = a.reshape((b, t // ML_CHUNK, ML_CHUNK) + a.shape[2:])
    a = jnp.moveaxis(a, 3, 2)
    return jnp.moveaxis(a, 1, 0)


def _zero_state(bsz):
    f32 = jnp.float32
    return (jnp.zeros((bsz, ML_HEADS, ML_DQK, ML_DV), f32),
            jnp.zeros((bsz, ML_HEADS, ML_DQK), f32),
            jnp.zeros((bsz, ML_HEADS), f32))


def mlstm_chunkwise(q, k, v, ig, lf, state):
    bsz, t = q.shape[:2]
    f32 = jnp.float32
    qs, ks, vs = (_to_chunks(a.astype(f32)) for a in (q, k, v))
    igs, lfs = _to_chunks(ig), _to_chunks(lf)
    earlier = np.tril(np.ones((ML_CHUNK, ML_CHUNK), dtype=bool))

    def body(carry, inp):
        C, n, m = carry
        qc, kc, vc, igc, lfc = inp
        b = jnp.cumsum(lfc, axis=-1)
        a = b + m[..., None]
        dmat = b[..., :, None] - b[..., None, :] + igc[..., None, :]
        dmat = jnp.where(earlier, dmat, -jnp.inf)
        m_s = jnp.maximum(a, jnp.max(dmat, axis=-1))
        w_state = jnp.exp(a - m_s)
        s = jnp.einsum('bhsd,bhjd->bhsj', qc, kc) * jnp.exp(dmat - m_s[..., None])
        num = w_state[..., None] * jnp.einsum('bhsd,bhde->bhse', qc, C) + jnp.einsum('bhsj,bhje->bhse', s, vc)
        den = w_state * jnp.einsum('bhsd,bhd->bhs', qc, n) + jnp.sum(s, axis=-1)
        h = num / jnp.maximum(jnp.abs(den), jnp.exp(-m_s))[..., None]
        b_tot = b[..., -1]
        g = b_tot[..., None] - b + igc
        m_new = jnp.maximum(b_tot + m, jnp.max(g, axis=-1))
        decay = jnp.exp(b_tot + m - m_new)
        wk = jnp.exp(g - m_new[..., None])
        C_new = decay[..., None, None] * C + jnp.einsum('bhj,bhjd,bhje->bhde', wk, kc, vc)
        n_new = decay[..., None] * n + jnp.einsum('bhj,bhjd->bhd', wk, kc)
        return (C_new, n_new, m_new), h

    final, h = lax.scan(body, state, (qs, ks, vs, igs, lfs))
    h = jnp.moveaxis(jnp.moveaxis(h, 0, 1), 2, 3)
    return h.reshape(bsz, t, ML_HEADS, ML_DV), final


def _mlstm_project(z, w_in, b_gate, row, col):
    bsz, t, _ = z.shape
    p = z @ w_in
    q, k, v, og, g = jnp.split(p, [ML_QK_W, 2 * ML_QK_W, 2 * ML_QK_W + ML_V_W, 2 * ML_QK_W + 2 * ML_V_W], axis=-1)
    q = q.reshape(bsz, t, ML_HEADS, ML_DQK)
    k = k.reshape(bsz, t, ML_HEADS, ML_DQK)
    if row is not None:
        q = axial_rope(q, row, col)
        k = axial_rope(k, row, col)
    k = k * (ML_DQK ** -0.5)
    v = v.reshape(bsz, t, ML_HEADS, ML_DV)
    g = g.astype(jnp.float32) + b_gate.astype(jnp.float32)
    g = (GATE_CAP * jnp.tanh(g / GATE_CAP)).reshape(bsz, t, 4, ML_HEADS)
    fwd = (g[:, :, 0], jax.nn.log_sigmoid(g[:, :, 1]))
    bwd = (g[:, :, 2], jax.nn.log_sigmoid(g[:, :, 3]))
    return q, k, v, og, fwd, bwd


def _mlstm_readout(h, og, norm_g, w_out):
    bsz, t = h.shape[:2]
    hf = h * lax.rsqrt(jnp.mean(h * h, axis=-1, keepdims=True) + RMS_EPS)
    hf = hf.reshape(bsz, t, ML_V_W).astype(og.dtype) * norm_g
    return (hf * jax.nn.sigmoid(og)) @ w_out


def mlstm_mixer(u, uc, w_in, b_gate, norm_g, w_out, row, col, need_ctx):
    q, k, v, og, gf, gb = _mlstm_project(u, w_in, b_gate, row, col)
    qc, kc, vc, ogc, gcf, gcb = _mlstm_project(uc, w_in, b_gate, None, None)
    flip = lambda a: jnp.flip(a, axis=1)
    s0 = _zero_state(u.shape[0])
    hcf, scf = mlstm_chunkwise(qc, kc, vc, gcf[0], gcf[1], s0)
    hf, _ = mlstm_chunkwise(q, k, v, gf[0], gf[1], scf)
    hcb, scb = mlstm_chunkwise(flip(qc), flip(kc), flip(vc), flip(gcb[0]), flip(gcb[1]), s0)
    hb, _ = mlstm_chunkwise(flip(q), flip(k), flip(v), flip(gb[0]), flip(gb[1]), scb)
    y = _mlstm_readout(hf + flip(hb), og, norm_g, w_out)
    if not need_ctx:
        return y, None
    yc = _mlstm_readout(hcf + flip(hcb), ogc, norm_g, w_out)
    return y, yc


def neighbourhood_attention(q, k, v, kc, vc, rpb, n_rows):
    bsz, t, h, dh = q.shape
    kh = min(NA_KH, n_rows)
    kw = NA_KW
    scale = dh ** -0.5
    qg = q.reshape(bsz, n_rows, GRID_W, h, dh)
    kg = k.reshape(bsz, n_rows, GRID_W, h, dh)
    vg = v.reshape(bsz, n_rows, GRID_W, h, dh)
    cols = np.arange(GRID_W)
    c_start = np.clip(cols - kw // 2, 0, GRID_W - kw)
    cidx = c_start[:, None] + np.arange(kw)[None, :]
    col_off = cidx - cols[:, None] + (NA_KW - 1)
    rpb_col = rpb[:, :, col_off]

    def row_block(r):
        r_start = jnp.clip(r - kh // 2, 0, n_rows - kh)
        qr = lax.dynamic_index_in_dim(qg, r, axis=1, keepdims=False)
        kband = lax.dynamic_slice_in_dim(kg, r_start, kh, axis=1)
        vband = lax.dynamic_slice_in_dim(vg, r_start, kh, axis=1)
        kwin = kband[:, :, cidx]
        vwin = jnp.transpose(vband[:, :, cidx], (0, 2, 1, 3, 4, 5)).reshape(bsz, GRID_W, kh * kw, h, dh)
        row_off = r_start + jnp.arange(kh) - r + (NA_KH - 1)
        bias = jnp.transpose(jnp.take(rpb_col, row_off, axis=1), (0, 2, 1, 3))
        s_win = jnp.einsum('bqhd,brqwhd->bhqrw', qr, kwin) * scale + bias[None]
        s_ctx = jnp.einsum('bqhd,bchd->bhqc', qr, kc) * scale
        s = jnp.concatenate([s_win.reshape(bsz, h, GRID_W, kh * kw), s_ctx], axis=-1)
        p = jax.nn.softmax(s.astype(jnp.float32), axis=-1).astype(v.dtype)
        o = jnp.einsum('bhqn,bqnhd->bqhd', p[..., :kh * kw], vwin)
        return o + jnp.einsum('bhqc,bchd->bqhd', p[..., kh * kw:], vc)

    out = lax.map(row_block, jnp.arange(n_rows))
    return jnp.transpose(out, (1, 0, 2, 3, 4)).reshape(bsz, t, h, dh)


def na_mixer(u, uc, w_qkv, rpb, w_out, n_rows, need_ctx):
    bsz, t, _ = u.shape
    lc = uc.shape[1]
    p = (u @ w_qkv).reshape(bsz, t, 3, NA_HEADS, NA_HEAD_DIM)
    pc = (uc @ w_qkv).reshape(bsz, lc, 3, NA_HEADS, NA_HEAD_DIM)
    kc, vc = pc[:, :, 1], pc[:, :, 2]
    o = neighbourhood_attention(p[:, :, 0], p[:, :, 1], p[:, :, 2], kc, vc, rpb, n_rows)
    y = o.reshape(bsz, t, D_MODEL) @ w_out
    if not need_ctx:
        return y, None
    s = jnp.einsum('bqhd,bkhd->bhqk', pc[:, :, 0], kc) * (NA_HEAD_DIM ** -0.5)
    pr = jax.nn.softmax(s.astype(jnp.float32), axis=-1).astype(vc.dtype)
    yc = jnp.einsum('bhqk,bkhd->bqhd', pr, vc).reshape(bsz, lc, D_MODEL) @ w_out
    return y, yc


def sq_relu_mlp(u, w1, w2):
    return jnp.square(jax.nn.relu(u @ w1)) @ w2


def setup_inputs(seed: int = 0) -> dict:
    key = jax.random.key(seed)
    ks = jax.random.split(key, 18)
    f32 = jnp.float32

    def nrm(k, shape, scale):
        return jax.random.normal(k, shape, f32) * scale

    f_bias = jnp.linspace(FGATE_BIAS_LO, FGATE_BIAS_HI, ML_HEADS, dtype=f32)
    zeros_h = jnp.zeros((ML_HEADS,), f32)
    gate_off = jnp.stack([zeros_h, f_bias, zeros_h, f_bias], axis=0)
    ml_b_gate = (nrm(ks[9], (N_LAYERS_A, 4, ML_HEADS), 0.1) + gate_off).reshape(N_LAYERS_A, 4 * ML_HEADS)
    return {
        'x': nrm(ks[0], (BATCH, SEQ, D_MODEL), 1.0),
        'c': nrm(ks[1], (BATCH, D_MODEL), 1.0),
        'ctx': nrm(ks[2], (BATCH, CTX_LEN, D_MODEL), 1.0),
        'c_ctx': nrm(ks[3], (D_MODEL,), 1.0),
        'ada_w': nrm(ks[4], (DEPTH, D_MODEL, N_MOD * D_MODEL), 0.5 * D_MODEL ** -0.5),
        'ada_b': nrm(ks[5], (DEPTH, N_MOD * D_MODEL), 0.02),
        'norm_mix': 1.0 + nrm(ks[6], (DEPTH, D_MODEL), 0.02),
        'norm_mlp': 1.0 + nrm(ks[7], (DEPTH, D_MODEL), 0.02),
        'ml_w_in': nrm(ks[8], (N_LAYERS_A, D_MODEL, ML_IN_W), D_MODEL ** -0.5),
        'ml_b_gate': ml_b_gate,
        'ml_norm': 1.0 + nrm(ks[10], (N_LAYERS_A, ML_V_W), 0.02),
        'ml_w_out': nrm(ks[11], (N_LAYERS_A, ML_V_W, D_MODEL), ML_V_W ** -0.5),
        'na_w_qkv': nrm(ks[12], (N_LAYERS_B, D_MODEL, 3 * D_MODEL), D_MODEL ** -0.5),
        'na_rpb': nrm(ks[13], (N_LAYERS_B, NA_HEADS, 2 * NA_KH - 1, 2 * NA_KW - 1), 0.5),
        'na_w_out': nrm(ks[14], (N_LAYERS_B, D_MODEL, D_MODEL), D_MODEL ** -0.5),
        'mlp_w1': nrm(ks[15], (DEPTH, D_MODEL, D_FF), D_MODEL ** -0.5),
        'mlp_w2': nrm(ks[16], (DEPTH, D_FF, D_MODEL), D_FF ** -0.5),
        'final_norm': 1.0 + nrm(ks[17], (D_MODEL,), 0.02),
    }


def reference(x, c, ctx, c_ctx, ada_w, ada_b, norm_mix, norm_mlp, ml_w_in, ml_b_gate, ml_norm, ml_w_out,
              na_w_qkv, na_rpb, na_w_out, mlp_w1, mlp_w2, final_norm):
    t = x.shape[1]
    n_rows = t // GRID_W
    pos = jnp.arange(t, dtype=jnp.int32)
    row, col = pos // GRID_W, pos % GRID_W
    cond_lat = jax.nn.silu(c)
    cond_ctx = jax.nn.silu(c_ctx)
    h, hc = x, ctx
    for i in range(DEPTH):
        need_ctx = i < DEPTH - 1
        mx = (cond_lat @ ada_w[i] + ada_b[i])[:, None, :]
        mc = cond_ctx @ ada_w[i] + ada_b[i]
        sh1, sc1, g1, sh2, sc2, g2 = jnp.split(mx, N_MOD, axis=-1)
        csh1, csc1, cg1, csh2, csc2, cg2 = jnp.split(mc, N_MOD, axis=-1)
        u = rmsnorm(h, norm_mix[i]) * (1 + sc1) + sh1
        uc = rmsnorm(hc, norm_mix[i]) * (1 + csc1) + csh1
        j = i // N_MIXERS
        if i % N_MIXERS == 0:
            y, yc = mlstm_mixer(u, uc, ml_w_in[j], ml_b_gate[j], ml_norm[j], ml_w_out[j], row, col, need_ctx)
        else:
            y, yc = na_mixer(u, uc, na_w_qkv[j], na_rpb[j], na_w_out[j], n_rows, need_ctx)
        h = h + g1 * y
        u = rmsnorm(h, norm_mlp[i]) * (1 + sc2) + sh2
        h = h + g2 * sq_relu_mlp(u, mlp_w1[i], mlp_w2[i])
        if need_ctx:
            hc = hc + cg1 * yc
            uc = rmsnorm(hc, norm_mlp[i]) * (1 + csc2) + csh2
            hc = hc + cg2 * sq_relu_mlp(uc, mlp_w1[i], mlp_w2[i])
    return rmsnorm(h, final_norm)
```

```python
import math
from contextlib import ExitStack
import numpy as np
import concourse.bass as bass
import concourse.mybir as mybir
from concourse.bass_utils import run_bass_kernel_spmd

F32 = mybir.dt.float32
BF16 = mybir.dt.bfloat16
AF = mybir.ActivationFunctionType
ALU = mybir.AluOpType
AX = mybir.AxisListType

D = 2048
T = 4096
LC = 256
NT = (T + LC) // 128
NTOK = T + LC
DEPTH = 4
DFF = 8192
ML_IN = 6176
EPS = 1e-6
NEG = -30000.0
PROFILE_SCOPES = False


class Res:
    __slots__ = ("w", "r", "chan")

    def __init__(self):
        self.w = {}
        self.r = {}
        self.chan = None


class Sched:
    ENGS = ("pe", "act", "dve", "pool", "sp")

    def __init__(self, nc, es, nchan=40):
        self.nc = nc
        self.semobj = {}
        self.cnt = {}
        for e in ("pe", "act", "dve", "pool"):
            self.semobj[e] = es.enter_context(nc.semaphore("sem_" + e))
            self.cnt[e] = 0
        self.chans = []
        for i in range(nchan):
            k = "ch%d" % i
            self.semobj[k] = es.enter_context(nc.semaphore("sem_" + k))
            self.cnt[k] = 0
            self.chans.append(k)
        self.next_chan = 0
        self.q = {e: [] for e in self.ENGS}
        self.waited = {e: {} for e in self.ENGS}
        self.ninstr = 0

    def new_stage(self):
        self.next_chan = 8

    def _chan(self, res):
        if res.chan is None or res.chan[1] != id(self.q):
            pass
        if res.chan is None:
            assert self.next_chan < len(self.chans), "out of dma channels"
            res.chan = self.chans[self.next_chan]
            self.next_chan += 1
        return res.chan

    def _waits(self, eng, reads, writes, extra=()):
        best = {}
        wd = self.waited[eng]
        for r in reads:
            for k, v in r.w.items():
                if v > best.get(k, 0):
                    best[k] = v
        for w in writes:
            for k, v in w.w.items():
                if v > best.get(k, 0):
                    best[k] = v
            for k, v in w.r.items():
                if v > best.get(k, 0):
                    best[k] = v
        for k, v in extra:
            if v > best.get(k, 0):
                best[k] = v
        out = []
        for k, v in best.items():
            if wd.get(k, 0) < v:
                wd[k] = v
                out.append((k, v))
        return out

    def _commit(self, tok, reads, writes):
        k, v = tok
        for r in reads:
            if r.r.get(k, 0) < v:
                r.r[k] = v
        for w in writes:
            if w.w.get(k, 0) < v:
                w.w[k] = v

    def op(self, eng, fn, reads=(), writes=()):
        waits = self._waits(eng, reads, writes)
        self.cnt[eng] += 1
        tok = (eng, self.cnt[eng])
        if eng != "pe":
            self.waited[eng][eng] = max(self.waited[eng].get(eng, 0), 0)
        self.q[eng].append((waits, fn, (eng, 1)))
        self._commit(tok, reads, writes)
        self.ninstr += 1
        return tok

    def mm(self, fns, reads=(), writes=()):
        waits = [w for w in self._waits("pe", reads, writes) if w[0] != "pe"]
        self.cnt["pe"] += 1
        tok = ("pe", self.cnt["pe"])
        n = len(fns)
        for i, fn in enumerate(fns):
            self.q["pe"].append((waits if i == 0 else (), fn, ("pe", 1) if i == n - 1 else None))
        self._commit(tok, reads, writes)
        self.ninstr += n
        return tok

    def dma(self, queue, owner, fns, reads=(), writes=()):
        ch = self._chan(owner)
        waits = self._waits(queue, reads, writes, extra=((ch, self.cnt[ch]),))
        self.cnt[ch] += 16 * len(fns)
        tok = (ch, self.cnt[ch])
        for i, fn in enumerate(fns):
            self.q[queue].append((waits if i == 0 else (), fn, (ch, 16)))
        self._commit(tok, reads, writes)
        self.ninstr += len(fns)
        return tok

    def barrier(self):
        reserved = set(self.chans[0:8])
        allk = [(k, v) for k, v in self.cnt.items() if v > 0 and k not in reserved]
        for e in self.ENGS:
            wd = self.waited[e]
            ws = []
            for k, v in allk:
                if wd.get(k, 0) < v:
                    wd[k] = v
                    ws.append((k, v))
            if ws:
                self.q[e].append((ws, None, None))

    def flush(self, name=None):
        nc = self.nc
        if not any(self.q[e] for e in self.ENGS):
            return
        with ExitStack() as _es:
            if name is not None and PROFILE_SCOPES:
                _es.enter_context(nc.named_scope(name))
            block = _es.enter_context(nc.Block())
            decos = {"pe": block.tensor, "act": block.scalar, "dve": block.vector, "pool": block.gpsimd,
                     "sp": block.sync}
            for e in self.ENGS:
                items = self.q[e]
                if not items:
                    continue
                semobj = self.semobj

                def body(eng, items=items):
                    for waits, fn, inc in items:
                        for k, v in waits:
                            eng.wait_ge(semobj[k], v)
                        if fn is None:
                            continue
                        ins = fn(eng)
                        if inc is not None:
                            ins.then_inc(semobj[inc[0]], inc[1])

                decos[e](body)
                self.q[e] = []


def TT(out, in0, in1, op):
    return lambda e: e.tensor_tensor(out=out, in0=in0, in1=in1, op=op)


def STT(out, in0, scalar, in1, op0, op1):
    return lambda e: e.scalar_tensor_tensor(out=out, in0=in0, scalar=scalar, in1=in1, op0=op0, op1=op1)


def TS(out, in0, s1, s2, op0, op1=None):
    if op1 is None:
        return lambda e: e.tensor_scalar(out=out, in0=in0, scalar1=s1, scalar2=None, op0=op0)
    return lambda e: e.tensor_scalar(out=out, in0=in0, scalar1=s1, scalar2=s2, op0=op0, op1=op1)


def ACT(out, in_, func, bias=None, scale=None, accum_out=None):
    kw = {}
    if bias is not None:
        kw["bias"] = bias
    if scale is not None:
        kw["scale"] = scale
    if accum_out is not None:
        kw["accum_out"] = accum_out
    return lambda e: e.activation(out=out, in_=in_, func=func, **kw)


def CP(out, in_):
    return lambda e: e.tensor_copy(out=out, in_=in_)


def RCP(out, in_):
    return lambda e: e.reciprocal(out=out, in_=in_)


def MM(out, lhsT, rhs, start, stop):
    return lambda e: e.matmul(out, lhsT=lhsT, rhs=rhs, start=start, stop=stop)


def TR(out, in_, ident):
    return lambda e: e.transpose(out, in_, ident)


def DMA(out, in_):
    return lambda e: e.dma_start(out=out, in_=in_)


def MEMSET(ap, v):
    return lambda e: e.memset(ap, v)


def RED(out, in_, op=None):
    return lambda e: e.tensor_reduce(out=out, in_=in_, axis=AX.X, op=ALU.add if op is None else op)


class _Stop(Exception):
    pass


class Prog:
    def check(self, name):
        if self.stop_after == name:
            raise _Stop()

    def __init__(self, stop_after=None, debug=()):
        self.stop_after = stop_after
        self.debug = debug
        self.nc = bass.Bass("TRN2", target_bir_lowering=False)
        self.es = ExitStack()
        self.stages_done = []

    def din(self, name, shape, dt=F32):
        return self.nc.dram_tensor(name, list(shape), dt, kind="ExternalInput").ap()

    def dscr(self, name, shape, dt=F32):
        return self.nc.dram_tensor(name, list(shape), dt, kind="Internal").ap()

    def dout(self, name, shape, dt=F32):
        return self.nc.dram_tensor(name, list(shape), dt, kind="ExternalOutput").ap()

    def sb(self, st, name, shape, dt):
        self._uid = getattr(self, "_uid", 0) + 1
        return st.enter_context(self.nc.sbuf_tensor("%s_u%d" % (name, self._uid), list(shape), dt))

    def stage_begin(self):
        self.S.new_stage()
        self.S.barrier()
        return ExitStack()

    def stage_end(self, st, name):
        self.S.barrier()
        self.S.flush(name)
        st.close()
        self.stages_done.append(name)

    def build(self):
        nc = self.nc
        es = self.es
        I = {}
        I["x"] = self.din("x", [T, D])
        I["ctx"] = self.din("ctx", [LC, D])
        I["cond"] = self.din("cond", [128, 32])
        I["ada_w"] = self.din("ada_w", [DEPTH, D, 6 * D])
        I["ada_b"] = self.din("ada_b", [DEPTH, 6 * D])
        I["norm_mix"] = self.din("norm_mix", [DEPTH, D])
        I["norm_mlp"] = self.din("norm_mlp", [DEPTH, D])
        I["ml_w_in"] = self.din("ml_w_in", [2, D, ML_IN])
        I["ml_b_gate"] = self.din("ml_b_gate", [2, 32])
        I["ml_norm"] = self.din("ml_norm", [2, D])
        I["ml_w_out"] = self.din("ml_w_out", [2, D, D])
        I["na_w_qkv"] = self.din("na_w_qkv", [2, D, 3 * D])
        I["na_bias"] = self.din("na_bias", [2, 16, 128, 21 * 128])
        I["na_w_out"] = self.din("na_w_out", [2, D, D])
        I["mlp_w1"] = self.din("mlp_w1", [DEPTH, D, DFF])
        I["mlp_w2"] = self.din("mlp_w2", [DEPTH, DFF, D])
        I["final_norm"] = self.din("final_norm", [1, D])
        I["cmat"] = self.din("cmat", [128, 4 * 128])
        I["rope"] = self.din("rope", [T, 2, 512])
        self.I = I
        W = {}
        W["ml_w_in"] = self.dscr("b_ml_w_in", [2, D, ML_IN], BF16)
        W["ml_w_out"] = self.dscr("b_ml_w_out", [2, D, D], BF16)
        W["na_w_qkv"] = self.dscr("b_na_w_qkv", [2, D, 3 * D], BF16)
        W["na_w_out"] = self.dscr("b_na_w_out", [2, D, D], BF16)
        W["mlp_w1"] = self.dscr("b_mlp_w1", [DEPTH, D, DFF], BF16)
        W["mlp_w2"] = self.dscr("b_mlp_w2", [DEPTH, DFF, D], BF16)
        self.W = W
        self.Wres = {(k, l): Res() for k in W for l in range(4)}
        X = {}
        X["h"] = self.dscr("hbuf", [NTOK, D])
        X["uT"] = self.dscr("uT", [NT, 128, 16, 128], BF16)
        X["oT"] = self.dscr("oT", [NT, 128, 16, 128], BF16)
        X["modv"] = self.dscr("modv", [DEPTH, 2, 6, D])
        X["q_tm"] = self.dscr("q_tm", [NTOK, 1024])
        X["k_tm"] = self.dscr("k_tm", [NTOK, 1024])
        X["v_tm"] = self.dscr("v_tm", [NTOK, D], BF16)
        X["og_tm"] = self.dscr("og_tm", [NTOK, D])
        X["hf"] = self.dscr("hf", [NTOK, D])
        X["qT"] = self.dscr("qT", [16, 128, NTOK], BF16)
        X["kT"] = self.dscr("kT", [16, 128, NTOK], BF16)
        self.X = X
        self.Xres = {k: Res() for k in X}
        self.y = self.dout("y", [T, D])
        self.yres = Res()
        self.dbg = {}
        for name, shape, dt in self.debug:
            self.dbg[name] = self.dout("dbg_" + name, shape, dt)

        self.S = Sched(nc, es, nchan=48)
        self.ps = [es.enter_context(nc.psum_tensor("ps%d" % i, [128, 512], F32)) for i in range(8)]
        self.psres = [Res() for _ in range(8)]
        self.cm_f = es.enter_context(nc.sbuf_tensor("cm_f", [128, 4 * 128], F32))
        self.cm_b = es.enter_context(nc.sbuf_tensor("cm_b", [128, 2 * 128], BF16))
        self.cres = Res()
        self.ident = self.cm_b[:, 0:128]
        self.ones_bf = self.cm_b[:, 128:256]
        self.ones_f = self.cm_f[:, 128:256]
        self.maskf = self.cm_f[:, 256:384]
        self.maskb = self.cm_f[:, 384:512]

        try:
            self.stage_prologue()
            self.check("prologue")
            self.stage_mod()
            self.check("mod")
            for i in range(DEPTH):
                need_ctx = i < DEPTH - 1
                self.stage_norm(i, 0)
                self.check("norm%d" % i)
                if i % 2 == 0:
                    self.layer_mlstm(i)
                else:
                    self.layer_na(i, need_ctx)
                self.check("mix%d" % i)
                self.stage_wout_norm(i, need_ctx)
                self.check("wout%d" % i)
                self.stage_mlp(i, need_ctx)
                self.check("mlp%d" % i)
            self.stage_final()
        except _Stop:
            pass
        return self.finish()

    def finish(self):
        self.stage_debug()
        self.es.close()
        return self.nc

    def stage_debug(self):
        if not self.dbg:
            return
        S = self.S
        st = self.stage_begin()
        r = Res()
        for name, ap in self.dbg.items():
            src = self.X[name] if name in self.X else self.W[name]
            if len(src.shape) > 2:
                if name in self.W:
                    srcv = src[0]
                else:
                    srcv = src
            else:
                srcv = src
            S.dma("sp", r, [DMA(ap, srcv)], reads=[], writes=[r])
        self.stage_end(st, "debug")

    def stage_prologue(self):
        S, I, W, X = self.S, self.I, self.W, self.X
        st = self.stage_begin()
        S.dma("sp", self.cres, [DMA(self.cm_f[:], I["cmat"][:, :])], writes=[self.cres])
        S.op("dve", CP(self.cm_b[:], self.cm_f[:, 0:256]), reads=[self.cres], writes=[self.cres])
        hr = self.Xres["h"]
        r1, r2 = Res(), Res()
        S.dma("sp", r1, [DMA(X["h"][0:LC, :], I["ctx"][:, :])], writes=[hr])
        fns = [DMA(X["h"][LC + i * 512:LC + (i + 1) * 512, :], I["x"][i * 512:(i + 1) * 512, :]) for i in range(8)]
        S.dma("sp", r2, fns, writes=[hr])
        self.cast_res = [Res() for _ in range(8)]
        for i, r in enumerate(self.cast_res):
            r.chan = self.S.chans[i]
        self.cast_fifo = []
        self.cast_i = 0

        def cast(key, l, li):
            src = I[key][li]
            dst = W[key][li]
            rows = src.shape[0]
            step = 128
            for r0 in range(0, rows, step):
                self.cast_fifo.append((key, l, dst[r0:r0 + step, :], src[r0:r0 + step, :]))

        for l in range(DEPTH):
            j = l // 2
            if l % 2 == 0:
                cast("ml_w_in", l, j)
                cast("ml_w_out", l, j)
            else:
                cast("na_w_qkv", l, j)
                cast("na_w_out", l, j)
            cast("mlp_w1", l, l)
            cast("mlp_w2", l, l)
        self.ensure_casts("ml_w_in", 0)
        self.stage_end(st, "prologue")

    def pace_casts(self, n):
        for _ in range(n):
            if not self.cast_fifo:
                return
            key, l, dst, src = self.cast_fifo.pop(0)
            r = self.cast_res[self.cast_i % 8]
            self.cast_i += 1
            self.S.dma("pool", r, [DMA(dst, src)], writes=[self.Wres[(key, l)]])

    def ensure_casts(self, key, l):
        last = -1
        for i, job in enumerate(self.cast_fifo):
            if job[0] == key and job[1] == l:
                last = i
        self.pace_casts(last + 1)

    def stage_mod(self):
        S, I, X = self.S, self.I, self.X
        nc = self.nc
        st = self.stage_begin()
        cin = self.sb(st, "cin", [128, 32], F32)
        cT = self.sb(st, "cT", [128, 16, 2], F32)
        sig = self.sb(st, "csig", [128, 32], F32)
        modrow = self.sb(st, "modrow", [2, 6 * D], F32)
        brow = [self.sb(st, "brow%d" % i, [2, 512], F32) for i in range(2)]
        nrow = self.sb(st, "nrow", [2, 2, D], F32)
        orow = self.sb(st, "orow", [2, 6, D], F32)
        slabs = [self.sb(st, "aslab%d" % i, [128, 8, 512], F32) for i in range(4)]
        sres = [Res() for _ in range(4)]
        rc, rmod, rn, ro = Res(), Res(), Res(), Res()
        rb = [Res(), Res()]
        S.dma("sp", rc, [DMA(cin[:], I["cond"][:, :])], writes=[rc])
        S.op("act", ACT(sig[:], cin[:], AF.Sigmoid), reads=[rc], writes=[rmod])
        S.op("dve", TT(cT[:, :, 0], cin[:, 0:16], sig[:, 0:16], ALU.mult), reads=[rc, rmod], writes=[rc])
        S.op("dve", TT(cT[:, :, 1], cin[:, 16:32], sig[:, 16:32], ALU.mult), reads=[rc, rmod], writes=[rc])
        si = 0
        for l in range(DEPTH):
            S.dma("sp", rn, [DMA(nrow[0:1, 0, :], I["norm_mix"][l:l + 1, :]),
                             DMA(nrow[1:2, 0, :], I["norm_mix"][l:l + 1, :]),
                             DMA(nrow[0:1, 1, :], I["norm_mlp"][l:l + 1, :]),
                             DMA(nrow[1:2, 1, :], I["norm_mlp"][l:l + 1, :])], writes=[rn])
            wv = I["ada_w"][l].rearrange("(k p) n -> p k n", p=128)
            for cb in range(24):
                b = cb % 2
                cs = slice(cb * 512, (cb + 1) * 512)
                S.dma("sp", rb[b], [DMA(brow[b][0:1, :], I["ada_b"][l:l + 1, cs]),
                                    DMA(brow[b][1:2, :], I["ada_b"][l:l + 1, cs])], writes=[rb[b]])
                for kg in range(2):
                    s = si % 4
                    si += 1
                    S.dma("sp", sres[s], [DMA(slabs[s][:], wv[:, kg * 8:(kg + 1) * 8, cs])], writes=[sres[s]])
                    fns = [MM(self.ps[b][0:2, :], cT[:, kg * 8 + k, :], slabs[s][:, k, :], kg == 0 and k == 0,
                              kg == 1 and k == 7) for k in range(8)]
                    S.mm(fns, reads=[rc, sres[s]], writes=[self.psres[b]])
                S.op("dve", TT(modrow[:, cs], self.ps[b][0:2, :], brow[b][:], ALU.add),
                     reads=[self.psres[b], rb[b]], writes=[rmod])
            S.op("dve", STT(orow[:, 0, :], modrow[:, D:2 * D], 1.0, nrow[:, 0, :], ALU.add, ALU.mult),
                 reads=[rmod, rn], writes=[ro])
            S.op("dve", CP(orow[:, 1, :], modrow[:, 0:D]), reads=[rmod], writes=[ro])
            S.op("dve", CP(orow[:, 2, :], modrow[:, 2 * D:3 * D]), reads=[rmod], writes=[ro])
            S.op("dve", STT(orow[:, 3, :], modrow[:, 4 * D:5 * D], 1.0, nrow[:, 1, :], ALU.add, ALU.mult),
                 reads=[rmod, rn], writes=[ro])
            S.op("dve", CP(orow[:, 4, :], modrow[:, 3 * D:4 * D]), reads=[rmod], writes=[ro])
            S.op("dve", CP(orow[:, 5, :], modrow[:, 5 * D:6 * D]), reads=[rmod], writes=[ro])
            S.dma("sp", ro, [DMA(X["modv"][l], orow[:])], reads=[ro], writes=[self.Xres["modv"]])
        self.stage_end(st, "mod")

    def load_bc(self, st, name, vec_ap, res_src, width=D):
        t = self.sb(st, name, [128, width], F32)
        r = Res()
        self.S.dma("sp", r, [DMA(t[:], vec_ap.partition_broadcast(128))], reads=[res_src], writes=[r])
        return t, r

    def modvec(self, l, which, idx):
        return self.X["modv"][l, which, idx:idx + 1, :]

    def transposes_to(self, src_bf, src_res, dst, dst_res, nchunks=16):
        S = self.S
        for g in range(nchunks // 8):
            b = 6 + (self._trb % 2)
            self._trb += 1
            pb = self.ps[b][:].bitcast(BF16)
            fns = [TR(pb[:, i * 128:(i + 1) * 128], src_bf[:, (g * 8 + i) * 128:(g * 8 + i + 1) * 128], self.ident)
                   for i in range(8)]
            S.mm(fns, reads=[src_res, self.cres], writes=[self.psres[b]])
            eng = "act" if (self._trb % 2) else "dve"
            if eng == "act":
                S.op("act", ACT(dst[:, g * 8:(g + 1) * 8, :], pb.rearrange("p (a b) -> p a b", b=128), AF.Copy),
                     reads=[self.psres[b]], writes=[dst_res])
            else:
                S.op("dve", CP(dst[:, g * 8:(g + 1) * 8, :], pb.rearrange("p (a b) -> p a b", b=128)),
                     reads=[self.psres[b]], writes=[dst_res])

    _trb = 0

    def stage_norm(self, l, which, need_ctx=True):
        S, X = self.S, self.X
        st = self.stage_begin()
        ai, bi = (0, 1) if which == 0 else (3, 4)
        mres = self.Xres["modv"]
        A = [None, None]
        Bv = [None, None]
        for w in range(2):
            A[w] = self.load_bc(st, "nA%d" % w, self.modvec(l, w, ai), mres)
            Bv[w] = self.load_bc(st, "nB%d" % w, self.modvec(l, w, bi), mres)
        NS = 3
        hs = [self.sb(st, "nh%d" % i, [128, D], F32) for i in range(NS)]
        hres = [Res() for _ in range(NS)]
        junk = self.sb(st, "njunk", [128, D], BF16)
        jres = Res()
        t1 = [self.sb(st, "nt1_%d" % i, [128, D], F32) for i in range(2)]
        t1r = [Res() for _ in range(2)]
        ub = [self.sb(st, "nub%d" % i, [128, D], BF16) for i in range(2)]
        ubr = [Res() for _ in range(2)]
        uTt = [self.sb(st, "nuT%d" % i, [128, 16, 128], BF16) for i in range(2)]
        uTr = [Res() for _ in range(2)]
        stat = self.sb(st, "nstat", [128, NT, 4], F32)
        sres = [Res() for _ in range(NT)]
        S.op("pool", MEMSET(stat[:], 0.0), writes=sres)
        tiles = range(NT) if need_ctx else range(2, NT)
        for n, t in enumerate(tiles):
            w = 0 if t >= 2 else 1
            s = n % NS
            d2 = n % 2
            S.dma("sp", hres[s], [DMA(hs[s][:], X["h"][t * 128:(t + 1) * 128, :])], reads=[self.Xres["h"]],
                  writes=[hres[s]])
            S.op("act", ACT(junk[:], hs[s][:], AF.Square, accum_out=stat[:, t, 0:1]), reads=[hres[s]],
                 writes=[jres, sres[t]])
            S.op("act", ACT(stat[:, t, 1:2], stat[:, t, 0:1], AF.Sqrt, bias=EPS, scale=1.0 / D), reads=[sres[t]],
                 writes=[sres[t]])
            S.op("dve", RCP(stat[:, t, 2:3], stat[:, t, 1:2]), reads=[sres[t]], writes=[sres[t]])
            S.op("dve", STT(t1[d2][:], hs[s][:], stat[:, t, 2:3], A[w][0][:], ALU.mult, ALU.mult),
                 reads=[hres[s], sres[t], A[w][1]], writes=[t1r[d2]])
            S.op("dve", TT(ub[d2][:], t1[d2][:], Bv[w][0][:], ALU.add), reads=[t1r[d2], Bv[w][1]],
                 writes=[ubr[d2]])
            self.transposes_to(ub[d2], ubr[d2], uTt[d2], uTr[d2])
            S.dma("pool", uTr[d2], [DMA(X["uT"][t], uTt[d2][:])], reads=[uTr[d2]], writes=[self.Xres["uT"]])
        self.stage_end(st, "norm%d_%d" % (l, which))

    def blocks(self, need_ctx=True):
        bl = [(0, 2)] if need_ctx else []
        bl += [(2 + 4 * i, 4) for i in range(8)]
        return bl

    def gemm_setup(self, st, nslab=4, nut=2):
        g = {}
        g["slab"] = [self.sb(st, "gslab%d" % i, [128, 16, 512], BF16) for i in range(nslab)]
        g["slabr"] = [Res() for _ in range(nslab)]
        g["ut"] = [self.sb(st, "gut%d" % i, [128, 4, D], BF16) for i in range(nut)]
        g["utr"] = [Res() for _ in range(nut)]
        g["si"] = 0
        g["ui"] = 0
        g["bset"] = 0
        return g

    def load_ut(self, g, src, src_res, t0, nt):
        s = g["ui"] % len(g["ut"])
        g["ui"] += 1
        ut, utr = g["ut"][s], g["utr"][s]
        self.S.dma("sp", utr, [DMA(ut[:, 0:nt, :], src[t0:t0 + nt].rearrange("t p k c -> p t (k c)"))],
                   reads=[src_res], writes=[utr])
        return ut, utr

    def gemm_cols(self, g, mode, uts, nt, Wap, Wres, c0, cw, epi, nbank=4):
        S = self.S
        Wv = Wap.rearrange("(k p) n -> p k n", p=128)
        nkg = len(uts)
        bs = g["bset"] % (8 // nbank) if nbank <= 4 else 0
        g["bset"] += 1
        banks = [bs * nbank + i for i in range(nbank)]
        nout = nt if mode == "tm" else cw // 128
        for kg in range(nkg):
            s = g["si"] % len(g["slab"])
            g["si"] += 1
            slab, slr = g["slab"][s], g["slabr"][s]
            S.dma("sp", slr, [DMA(slab[:, :, 0:cw], Wv[:, kg * 16:(kg + 1) * 16, c0:c0 + cw])], reads=[Wres],
                  writes=[slr])
            ut, utr = uts[kg]
            for o in range(nout):
                b = banks[o]
                fns = []
                for k in range(16):
                    first = (kg == 0 and k == 0)
                    last = (kg == nkg - 1 and k == 15)
                    if mode == "tm":
                        fns.append(MM(self.ps[b][:, 0:cw], ut[:, o, k * 128:(k + 1) * 128], slab[:, k, 0:cw], first,
                                      last))
                    else:
                        fns.append(MM(self.ps[b][:, 0:nt * 128], slab[:, k, o * 128:(o + 1) * 128],
                                      ut[:, 0:nt, k * 128:(k + 1) * 128], first, last))
                S.mm(fns, reads=[utr, slr], writes=[self.psres[b]])
        for o in range(nout):
            epi(o, self.ps[banks[o]], self.psres[banks[o]])

    def stage_wout(self, l, need_ctx):
        S, X = self.S, self.X
        st = self.stage_begin()
        key = "ml_w_out" if l % 2 == 0 else "na_w_out"
        self.ensure_casts(key, l)
        Wap, Wres = self.W[key][l // 2], self.Wres[(key, l)]
        g = self.gemm_setup(st)
        G = [self.load_bc(st, "woG%d" % w, self.modvec(l, w, 2), self.Xres["modv"]) for w in range(2)]
        hs = [self.sb(st, "woh%d" % i, [128, D], F32) for i in range(6)]
        hres = [Res() for _ in range(6)]
        tmp = [self.sb(st, "wot%d" % i, [128, 512], F32) for i in range(3)]
        tres = [Res() for _ in range(3)]
        hi = 0
        ti = 0
        hr = self.Xres["h"]
        for (t0, nt) in self.blocks(need_ctx):
            w = 0 if t0 >= 2 else 1
            ut = self.load_ut(g, X["oT"], self.Xres["oT"], t0, nt)
            hmap = {}
            for o in range(nt):
                s = hi % 6
                hi += 1
                t = t0 + o
                S.dma("sp", hres[s], [DMA(hs[s][:], X["h"][t * 128:(t + 1) * 128, :])], reads=[hr],
                      writes=[hres[s]])
                hmap[o] = s
            for cb in range(4):
                def epi(o, ps, psr, cb=cb, hmap=hmap, w=w):
                    nonlocal ti
                    s = hmap[o]
                    x = ti % 3
                    ti += 1
                    sl = slice(cb * 512, (cb + 1) * 512)
                    S.op("dve", TT(tmp[x][:], ps[:, :], G[w][0][:, sl], ALU.mult), reads=[psr, G[w][1]],
                         writes=[tres[x]])
                    S.op("pool", TT(hs[s][:, sl], hs[s][:, sl], tmp[x][:], ALU.add), reads=[tres[x], hres[s]],
                         writes=[hres[s]])

                self.gemm_cols(g, "tm", [ut], nt, Wap, Wres, cb * 512, 512, epi)
            for o in range(nt):
                s = hmap[o]
                t = t0 + o
                S.dma("pool", hres[s], [DMA(X["h"][t * 128:(t + 1) * 128, :], hs[s][:])], reads=[hres[s]],
                      writes=[hr])
        self.stage_end(st, "wout%d" % l)


    def stage_wout_norm(self, l, need_ctx):
        S, X = self.S, self.X
        st = self.stage_begin()
        key = "ml_w_out" if l % 2 == 0 else "na_w_out"
        self.ensure_casts(key, l)
        Wap, Wres = self.W[key][l // 2], self.Wres[(key, l)]
        mres = self.Xres["modv"]
        Wsb = self.sb(st, "woW", [128, 16, D], BF16)
        Wr = Res()
        Wv = Wap.rearrange("(k p) n -> p k n", p=128)
        for q4 in range(4):
            S.dma("sp", Wr, [DMA(Wsb[:, q4 * 4:(q4 + 1) * 4, :], Wv[:, q4 * 4:(q4 + 1) * 4, :])], reads=[Wres],
                  writes=[Wr])
        G = [self.load_bc(st, "woG%d" % w, self.modvec(l, w, 2), mres) for w in range(2)]
        A = [self.load_bc(st, "woA%d" % w, self.modvec(l, w, 3), mres) for w in range(2)]
        Bv = [self.load_bc(st, "woB%d" % w, self.modvec(l, w, 4), mres) for w in range(2)]
        NS = 3
        uts = [self.sb(st, "wout%d" % i, [128, 16, 128], BF16) for i in range(NS)]
        utr = [Res() for _ in range(NS)]
        hs = [self.sb(st, "woh%d" % i, [128, D], F32) for i in range(NS)]
        hres = [Res() for _ in range(NS)]
        tmp = [self.sb(st, "wot%d" % i, [128, 512], F32) for i in range(4)]
        tres = [Res() for _ in range(4)]
        junk = self.sb(st, "wojunk", [128, D], BF16)
        jres = Res()
        t1 = [self.sb(st, "wot1_%d" % i, [128, D], F32) for i in range(2)]
        t1r = [Res() for _ in range(2)]
        ub = [self.sb(st, "woub%d" % i, [128, D], BF16) for i in range(2)]
        ubr = [Res() for _ in range(2)]
        uTt = [self.sb(st, "wouT%d" % i, [128, 16, 128], BF16) for i in range(2)]
        uTr = [Res() for _ in range(2)]
        stat = self.sb(st, "wostat", [128, NT, 4], F32)
        sres = [Res() for _ in range(NT)]
        S.op("pool", MEMSET(stat[:], 0.0), writes=sres)
        hr = self.Xres["h"]
        tiles = range(NT) if need_ctx else range(2, NT)
        ti = 0
        bi = 0
        for n, t in enumerate(tiles):
            w = 0 if t >= 2 else 1
            s = n % NS
            d2 = n % 2
            S.dma("sp", utr[s], [DMA(uts[s][:], X["oT"][t])], reads=[self.Xres["oT"]], writes=[utr[s]])
            S.dma("sp", hres[s], [DMA(hs[s][:], X["h"][t * 128:(t + 1) * 128, :])], reads=[hr], writes=[hres[s]])
            for cb in range(4):
                b = bi % 6
                bi += 1
                sl = slice(cb * 512, (cb + 1) * 512)
                fns = [MM(self.ps[b][:, :], uts[s][:, k, :], Wsb[:, k, sl], k == 0, k == 15) for k in range(16)]
                S.mm(fns, reads=[utr[s], Wr], writes=[self.psres[b]])
                x = ti % 4
                ti += 1
                S.op("dve", TT(tmp[x][:], self.ps[b][:, :], G[w][0][:, sl], ALU.mult),
                     reads=[self.psres[b], G[w][1]], writes=[tres[x]])
                S.op("pool" if cb % 2 == 0 else "dve", TT(hs[s][:, sl], hs[s][:, sl], tmp[x][:], ALU.add),
                     reads=[tres[x], hres[s]], writes=[hres[s]])
            S.dma("pool", hres[s], [DMA(X["h"][t * 128:(t + 1) * 128, :], hs[s][:])], reads=[hres[s]], writes=[hr])
            S.op("act", ACT(junk[:], hs[s][:], AF.Square, accum_out=stat[:, t, 0:1]), reads=[hres[s]],
                 writes=[jres, sres[t]])
            S.op("act", ACT(stat[:, t, 1:2], stat[:, t, 0:1], AF.Sqrt, bias=EPS, scale=1.0 / D), reads=[sres[t]],
                 writes=[sres[t]])
            S.op("dve", RCP(stat[:, t, 2:3], stat[:, t, 1:2]), reads=[sres[t]], writes=[sres[t]])
            S.op("dve", STT(t1[d2][:], hs[s][:], stat[:, t, 2:3], A[w][0][:], ALU.mult, ALU.mult),
                 reads=[hres[s], sres[t], A[w][1]], writes=[t1r[d2]])
            S.op("dve", TT(ub[d2][:], t1[d2][:], Bv[w][0][:], ALU.add), reads=[t1r[d2], Bv[w][1]],
                 writes=[ubr[d2]])
            self.transposes_to(ub[d2], ubr[d2], uTt[d2], uTr[d2])
            S.dma("pool", uTr[d2], [DMA(X["uT"][t], uTt[d2][:])], reads=[uTr[d2]], writes=[self.Xres["uT"]])
        self.stage_end(st, "woutn%d" % l)

    def stage_mlp(self, l, need_ctx):
        S, X = self.S, self.X
        st = self.stage_begin()
        self.ensure_casts("mlp_w2", l)
        W1, W1r = self.W["mlp_w1"][l], self.Wres[("mlp_w1", l)]
        W2, W2r = self.W["mlp_w2"][l], self.Wres[("mlp_w2", l)]
        g = self.gemm_setup(st, nslab=3, nut=2)
        G = [self.load_bc(st, "mlG%d" % w, self.modvec(l, w, 5), self.Xres["modv"]) for w in range(2)]
        aT = [self.sb(st, "aT%d" % i, [128, 4, D], BF16) for i in range(4)]
        aTr = [Res() for _ in range(4)]
        hs = [self.sb(st, "mlh%d" % i, [128, D], F32) for i in range(4)]
        hres = [Res() for _ in range(4)]
        tmp = [self.sb(st, "mlt%d" % i, [128, 512], F32) for i in range(3)]
        tres = [Res() for _ in range(3)]
        hi = 0
        ti = 0
        hr = self.Xres["h"]
        for (t0, nt) in self.blocks(need_ctx):
            w = 0 if t0 >= 2 else 1
            ut = self.load_ut(g, X["uT"], self.Xres["uT"], t0, nt)
            for cb in range(16):
                def epi_up(o, ps, psr, cb=cb):
                    nonlocal ti
                    x = ti % 3
                    ti += 1
                    fch = cb * 4 + o
                    kg, kk = fch // 16, fch % 16
                    n = nt * 128
                    S.op("act", ACT(tmp[x][:, 0:n], ps[:, 0:n], AF.Relu), reads=[psr], writes=[tres[x]])
                    dst = aT[kg][:, 0:nt, kk * 128:(kk + 1) * 128]
                    src = tmp[x][:, 0:n].rearrange("p (t c) -> p t c", c=128)
                    S.op("dve" if (ti % 2) else "pool", TT(dst, src, src, ALU.mult), reads=[tres[x]],
                         writes=[aTr[kg]])

                self.gemm_cols(g, "fm", [ut], nt, W1, W1r, cb * 512, 512, epi_up)
            hmap = {}
            for o in range(nt):
                s = hi % 4
                hi += 1
                t = t0 + o
                S.dma("sp", hres[s], [DMA(hs[s][:], X["h"][t * 128:(t + 1) * 128, :])], reads=[hr],
                      writes=[hres[s]])
                hmap[o] = s
            for cb in range(4):
                def epi_dn(o, ps, psr, cb=cb, hmap=hmap, w=w):
                    nonlocal ti
                    s = hmap[o]
                    x = ti % 3
                    ti += 1
                    sl = slice(cb * 512, (cb + 1) * 512)
                    S.op("dve", TT(tmp[x][:], ps[:, :], G[w][0][:, sl], ALU.mult), reads=[psr, G[w][1]],
                         writes=[tres[x]])
                    S.op("pool", TT(hs[s][:, sl], hs[s][:, sl], tmp[x][:], ALU.add), reads=[tres[x], hres[s]],
                         writes=[hres[s]])

                self.gemm_cols(g, "tm", [(aT[k], aTr[k]) for k in range(4)], nt, W2, W2r, cb * 512, 512, epi_dn)
            for o in range(nt):
                s = hmap[o]
                t = t0 + o
                S.dma("pool", hres[s], [DMA(X["h"][t * 128:(t + 1) * 128, :], hs[s][:])], reads=[hres[s]],
                      writes=[hr])
        self.stage_end(st, "mlp%d" % l)

    def stage_final(self):
        S, X = self.S, self.X
        st = self.stage_begin()
        rfn = Res()
        Gt, Gr = self.load_bc(st, "fnG", self.I["final_norm"][0:1, :], rfn)
        hs = [self.sb(st, "fh%d" % i, [128, D], F32) for i in range(3)]
        hres = [Res() for _ in range(3)]
        os_ = [self.sb(st, "fo%d" % i, [128, D], F32) for i in range(3)]
        ores = [Res() for _ in range(3)]
        junk = self.sb(st, "fjunk", [128, D], BF16)
        jres = Res()
        stat = self.sb(st, "fstat", [128, NT, 4], F32)
        sres = [Res() for _ in range(NT)]
        S.op("pool", MEMSET(stat[:], 0.0), writes=sres)
        for n, t in enumerate(range(2, NT)):
            s = n % 3
            S.dma("sp", hres[s], [DMA(hs[s][:], X["h"][t * 128:(t + 1) * 128, :])], reads=[self.Xres["h"]],
                  writes=[hres[s]])
            S.op("act", ACT(junk[:], hs[s][:], AF.Square, accum_out=stat[:, t, 0:1]), reads=[hres[s]],
                 writes=[jres, sres[t]])
            S.op("act", ACT(stat[:, t, 1:2], stat[:, t, 0:1], AF.Sqrt, bias=EPS, scale=1.0 / D), reads=[sres[t]],
                 writes=[sres[t]])
            S.op("dve", RCP(stat[:, t, 2:3], stat[:, t, 1:2]), reads=[sres[t]], writes=[sres[t]])
            S.op("dve", STT(os_[s][:], hs[s][:], stat[:, t, 2:3], Gt[:], ALU.mult, ALU.mult),
                 reads=[hres[s], sres[t], Gr], writes=[ores[s]])
            S.dma("pool", ores[s], [DMA(self.y[(t - 2) * 128:(t - 1) * 128, :], os_[s][:])], reads=[ores[s]],
                  writes=[self.yres])
        self.stage_end(st, "final")

    def layer_mlstm(self, l):
        lst = ExitStack()
        j = l // 2
        nc = self.nc
        self.GT = self.sb(lst, "GT", [128, NT, 32], F32)
        self.GTr = Res()
        self.EK = self.sb(lst, "EK", [128, 2, NT, 8], F32)
        self.EMB = self.sb(lst, "EMB", [128, 2, NT, 8], F32)
        self.WK = self.sb(lst, "WK", [128, 2, NT, 8], F32)
        self.EBT = self.sb(lst, "EBT", [128, 2, NT, 8], F32)
        self.gres = Res()
        try:
            self.ml_gin(l, j)
            self.check("gin%d" % l)
            self.ml_gates(l, j)
            self.check("gates%d" % l)
            self.ml_scan(l, j, 0)
            self.check("scanf%d" % l)
            self.ml_scan(l, j, 1)
        finally:
            self.S.barrier()
            self.S.flush()
            lst.close()

    def ml_gin(self, l, j):
        S, X, I = self.S, self.X, self.I
        st = self.stage_begin()
        self.ensure_casts("ml_w_in", l)
        Wap, Wres = self.W["ml_w_in"][j], self.Wres[("ml_w_in", l)]
        g = self.gemm_setup(st)
        rbg = Res()
        bg, bgr = self.load_bc(st, "bgbc", I["ml_b_gate"][j:j + 1, :], rbg, width=32)
        rope = [self.sb(st, "rope%d" % i, [128, 2, 512], F32) for i in range(8)]
        roper = [Res() for _ in range(8)]
        sf = [self.sb(st, "gsf%d" % i, [128, 512], F32) for i in range(4)]
        sfr = [Res() for _ in range(4)]
        ta = [self.sb(st, "gta%d" % i, [128, 512], F32) for i in range(2)]
        tar = [Res() for _ in range(2)]
        tb = [self.sb(st, "gtb%d" % i, [128, 512], F32) for i in range(2)]
        tbr = [Res() for _ in range(2)]
        sbf = [self.sb(st, "gsb%d" % i, [128, 512], BF16) for i in range(3)]
        sbr = [Res() for _ in range(3)]
        cnt = {"f": 0, "b": 0, "r": 0, "t": 0}
        rconst = Res()
        for (t0, nt) in self.blocks(True):
            ut = self.load_ut(g, X["uT"], self.Xres["uT"], t0, nt)
            rmap = {}
            if t0 >= 2:
                for o in range(nt):
                    s = cnt["r"] % 8
                    cnt["r"] += 1
                    t = t0 + o
                    S.dma("sp", roper[s], [DMA(rope[s][:], I["rope"][(t - 2) * 128:(t - 1) * 128, :, :])],
                          reads=[rconst], writes=[roper[s]])
                    rmap[o] = s
            for cb in range(13):
                c0 = cb * 512
                cw = 512 if cb < 12 else 32

                def epi(o, ps, psr, cb=cb, t0=t0, rmap=rmap):
                    t = t0 + o
                    rows = slice(t * 128, (t + 1) * 128)
                    if cb < 4:
                        dst = X["q_tm"] if cb < 2 else X["k_tm"]
                        dres = self.Xres["q_tm"] if cb < 2 else self.Xres["k_tm"]
                        s = cnt["f"] % 4
                        cnt["f"] += 1
                        if t >= 2:
                            rs = rmap[o]
                            x = cnt["t"] % 2
                            cnt["t"] += 1
                            S.op("dve", TT(ta[x][:], ps[:, :], rope[rs][:, 0, :], ALU.mult), reads=[psr, roper[rs]],
                                 writes=[tar[x]])
                            pv = ps[:, :].rearrange("p (g a c) -> p g a c", a=2, c=32)
                            sv = rope[rs][:, 1, :].rearrange("p (g a c) -> p g a c", a=2, c=32)
                            tv = tb[x][:].rearrange("p (g a c) -> p g a c", a=2, c=32)
                            S.op("dve", TT(tv[:, :, 0, :], pv[:, :, 1, :], sv[:, :, 0, :], ALU.mult),
                                 reads=[psr, roper[rs]], writes=[tbr[x]])
                            S.op("dve", TT(tv[:, :, 1, :], pv[:, :, 0, :], sv[:, :, 1, :], ALU.mult),
                                 reads=[psr, roper[rs]], writes=[tbr[x]])
                            S.op("pool", TT(sf[s][:], ta[x][:], tb[x][:], ALU.add), reads=[tar[x], tbr[x]],
                                 writes=[sfr[s]])
                        else:
                            S.op("act", ACT(sf[s][:], ps[:, :], AF.Copy), reads=[psr], writes=[sfr[s]])
                        cc = (cb % 2) * 512
                        S.dma("pool", sfr[s], [DMA(dst[rows, cc:cc + 512], sf[s][:])], reads=[sfr[s]],
                              writes=[dres])
                    elif cb < 8:
                        s = cnt["b"] % 3
                        cnt["b"] += 1
                        S.op("act", ACT(sbf[s][:], ps[:, :], AF.Copy), reads=[psr], writes=[sbr[s]])
                        cc = (cb - 4) * 512
                        S.dma("pool", sbr[s], [DMA(X["v_tm"][rows, cc:cc + 512], sbf[s][:])], reads=[sbr[s]],
                              writes=[self.Xres["v_tm"]])
                    elif cb < 12:
                        s = cnt["f"] % 4
                        cnt["f"] += 1
                        S.op("act", ACT(sf[s][:], ps[:, :], AF.Sigmoid), reads=[psr], writes=[sfr[s]])
                        cc = (cb - 8) * 512
                        S.dma("pool", sfr[s], [DMA(X["og_tm"][rows, cc:cc + 512], sf[s][:])], reads=[sfr[s]],
                              writes=[self.Xres["og_tm"]])
                    else:
                        S.op("dve", TT(self.GT[:, t, :], ps[:, 0:32], bg[:], ALU.add), reads=[psr, bgr],
                             writes=[self.GTr])

                self.gemm_cols(g, "tm", [ut], nt, Wap, Wres, c0, cw, epi)
        self.stage_end(st, "gin%d" % l)

    def ml_gates(self, l, j):
        S = self.S
        st = self.stage_begin()
        TH = self.sb(st, "TH", [128, NT, 32], F32)
        IG = self.sb(st, "IG", [128, 2, NT, 8], F32)
        E1 = self.sb(st, "E1", [128, 2, NT, 8], F32)
        LF = self.sb(st, "LF", [128, 2, NT, 8], F32)
        Bs = self.sb(st, "Bs", [128, 2, NT, 8], F32)
        BTs = self.sb(st, "BTs", [128, 2, NT, 8], F32)
        t1 = self.sb(st, "gt1", [128, 2, NT, 8], F32)
        t2 = self.sb(st, "gt2", [128, 2, NT, 8], F32)
        r = Res()
        lns = math.log(128.0 ** -0.5)
        lnsb = self.sb(st, "lnsb", [128, 1], F32)
        S.op("dve", MEMSET(lnsb[:], lns), writes=[r])
        S.op("act", ACT(TH[:], self.GT[:], AF.Tanh, scale=1.0 / 15.0), reads=[self.GTr], writes=[r])
        THv = TH[:].rearrange("p t (d a h) -> p t d a h", d=2, a=2)
        for d in range(2):
            S.op("dve", TS(IG[:, d], THv[:, :, d, 0, :], 15.0, None, ALU.mult), reads=[r], writes=[r])
            S.op("act", ACT(E1[:, d], THv[:, :, d, 1, :], AF.Exp, scale=-15.0), reads=[r], writes=[r])
        S.op("act", ACT(E1[:], E1[:], AF.Ln, bias=1.0), reads=[r], writes=[r])
        S.op("dve", TS(LF[:], E1[:], -1.0, None, ALU.mult), reads=[r], writes=[r])
        n = NT * 8
        for d in range(2):
            lfv = LF[:, d].rearrange("p t h -> p (t h)")
            tri = self.maskf if d == 0 else self.maskb
            S.mm([MM(self.ps[d][:, 0:n], tri, lfv, True, True)], reads=[r, self.cres], writes=[self.psres[d]])
            S.mm([MM(self.ps[2 + d][:, 0:n], self.ones_f, lfv, True, True)], reads=[r, self.cres],
                 writes=[self.psres[2 + d]])
            S.op("dve", CP(Bs[:, d].rearrange("p t h -> p (t h)"), self.ps[d][:, 0:n]), reads=[self.psres[d]],
                 writes=[r])
            S.op("dve", CP(BTs[:, d].rearrange("p t h -> p (t h)"), self.ps[2 + d][:, 0:n]),
                 reads=[self.psres[2 + d]], writes=[r])
        S.op("dve", TT(t1[:], IG[:], Bs[:], ALU.subtract), reads=[r], writes=[r])
        S.op("dve", TT(t2[:], t1[:], BTs[:], ALU.add), reads=[r], writes=[r])
        S.op("act", ACT(self.EK[:], t1[:], AF.Exp, bias=lnsb[:]), reads=[r], writes=[self.gres])
        S.op("act", ACT(self.WK[:], t2[:], AF.Exp, bias=lnsb[:]), reads=[r], writes=[self.gres])
        S.op("act", ACT(self.EMB[:], Bs[:], AF.Exp, scale=-1.0), reads=[r], writes=[self.gres])
        S.op("act", ACT(self.EBT[:], BTs[:], AF.Exp), reads=[r], writes=[self.gres])
        self.stage_end(st, "gates%d" % l)

    def ml_scan(self, l, j, d):
        S, X, I = self.S, self.X, self.I
        st = self.stage_begin()
        NSL = 2
        q32 = [self.sb(st, "q32_%d" % i, [128, 1024], F32) for i in range(NSL)]
        k32 = [self.sb(st, "k32_%d" % i, [128, 1024], F32) for i in range(NSL)]
        qb = [self.sb(st, "qb_%d" % i, [128, 1024], BF16) for i in range(NSL)]
        kb = [self.sb(st, "kb_%d" % i, [128, 1024], BF16) for i in range(NSL)]
        kh = [self.sb(st, "kh_%d" % i, [128, 8, 128], BF16) for i in range(NSL)]
        VA = [self.sb(st, "VA_%d" % i, [128, 8, 264], BF16) for i in range(NSL)]
        QT = [self.sb(st, "QT_%d" % i, [128, 8, 128], BF16) for i in range(NSL)]
        KT = [self.sb(st, "KT_%d" % i, [128, 8, 128], BF16) for i in range(NSL)]
        PT = [self.sb(st, "PT_%d" % i, [128, 8, 128], BF16) for i in range(NSL)]
        H = [self.sb(st, "H_%d" % i, [128, D], F32) for i in range(NSL)]
        R = {n: [Res() for _ in range(NSL)] for n in ("q32", "k32", "qb", "kb", "kh", "VA", "QT", "KT", "H")}
        PTr = [[Res() for _ in range(8)] for _ in range(NSL)]
        C = self.sb(st, "Cst", [128, 8, 264], F32)
        Cb = self.sb(st, "Cbf", [128, 8, 264], BF16)
        Cr = [Res() for _ in range(8)]
        Cbr = [Res() for _ in range(8)]
        dn = self.sb(st, "dn", [128, 8, 4], F32)
        dnr = [Res() for _ in range(8)]
        mask = self.maskf if d == 0 else self.maskb
        for h in range(8):
            S.op("pool", MEMSET(C[:, h, :], 0.0), writes=[Cr[h]])
            S.op("pool", MEMSET(Cb[:, h, :], 0.0), writes=[Cbr[h]])
        for s in range(NSL):
            S.op("pool", MEMSET(VA[s][:, :, 256:257], 1.0), writes=[R["VA"][s]])
        if d == 1:
            HF = [self.sb(st, "HF_%d" % i, [128, D], F32) for i in range(2)]
            HFr = [Res() for _ in range(2)]
            OG = [self.sb(st, "OG_%d" % i, [128, D], F32) for i in range(2)]
            OGr = [Res() for _ in range(2)]
            SQ = self.sb(st, "SQ", [128, D], BF16)
            SQr = Res()
            HN = self.sb(st, "HN", [128, D], F32)
            HNr = Res()
            OB = [self.sb(st, "OB_%d" % i, [128, D], BF16) for i in range(2)]
            OBr = [Res() for _ in range(2)]
            OT = [self.sb(st, "OTt_%d" % i, [128, 16, 128], BF16) for i in range(2)]
            OTr = [Res() for _ in range(2)]
            rs = self.sb(st, "rs", [128, NT, 8, 3], F32)
            rsr = Res()
            S.op("pool", MEMSET(rs[:], 0.0), writes=[rsr])
            rml = Res()
            MLN, MLNr = self.load_bc(st, "MLN", I["ml_norm"][j:j + 1, :], rml)
        order = list(range(NT)) if d == 0 else [1, 0] + list(range(NT - 1, 1, -1))
        tb_i = 0
        dc_i = 0
        for n, c in enumerate(order):
            self.pace_casts(2)
            s = n % NSL
            rows = slice(c * 128, (c + 1) * 128)
            S.dma("sp", R["q32"][s], [DMA(q32[s][:], X["q_tm"][rows, :])], reads=[self.Xres["q_tm"]],
                  writes=[R["q32"][s]])
            S.dma("sp", R["k32"][s], [DMA(k32[s][:], X["k_tm"][rows, :])], reads=[self.Xres["k_tm"]],
                  writes=[R["k32"][s]])
            S.dma("sp", R["VA"][s], [DMA(VA[s][:, :, 0:256], X["v_tm"][rows, :].rearrange("p (h e) -> p h e", e=256))],
                  reads=[self.Xres["v_tm"]], writes=[R["VA"][s]])
            if d == 1:
                S.dma("sp", HFr[s], [DMA(HF[s][:], X["hf"][rows, :])], reads=[self.Xres["hf"]], writes=[HFr[s]])
                S.dma("sp", OGr[s], [DMA(OG[s][:], X["og_tm"][rows, :])], reads=[self.Xres["og_tm"]],
                      writes=[OGr[s]])
            S.op("act", ACT(qb[s][:], q32[s][:], AF.Copy), reads=[R["q32"][s]], writes=[R["qb"][s]])
            S.op("dve", CP(kb[s][:], k32[s][:]), reads=[R["k32"][s]], writes=[R["kb"][s]])
            for h in range(8):
                S.op("act", ACT(kh[s][:, h, :], k32[s][:, h * 128:(h + 1) * 128], AF.Copy,
                                scale=self.WK[:, d, c, h:h + 1]), reads=[R["k32"][s], self.gres],
                     writes=[R["kh"][s]])
            self.transposes_to(qb[s], R["qb"][s], QT[s], R["QT"][s], nchunks=8)
            self.transposes_to(kb[s], R["kb"][s], KT[s], R["KT"][s], nchunks=8)
            for h in range(8):
                b = h // 4
                S.mm([MM(self.ps[b][:, (h % 4) * 128:(h % 4 + 1) * 128], KT[s][:, h, :], QT[s][:, h, :], True, True)],
                     reads=[R["KT"][s], R["QT"][s]], writes=[self.psres[b]])
            for h in range(8):
                b = h // 4
                S.op("dve", STT(PT[s][:, h, :], self.ps[b][:, (h % 4) * 128:(h % 4 + 1) * 128],
                                self.EK[:, d, c, h:h + 1], mask, ALU.mult, ALU.mult),
                     reads=[self.psres[b], self.gres, self.cres], writes=[PTr[s][h]])
            for h in range(8):
                tbk = 2 + (tb_i % 2)
                tb_i += 1
                S.mm([MM(self.ps[tbk][:, 0:257], QT[s][:, h, :], Cb[:, h, 0:257], True, False),
                      MM(self.ps[tbk][:, 0:257], PT[s][:, h, :], VA[s][:, h, 0:257], False, True)],
                     reads=[R["QT"][s], Cbr[h], PTr[s][h], R["VA"][s]], writes=[self.psres[tbk]])
                S.op("dve", TT(dn[:, h, 2:3], self.ps[tbk][:, 256:257], self.EMB[:, d, c, h:h + 1], ALU.max),
                     reads=[self.psres[tbk], self.gres], writes=[dnr[h]])
                S.op("dve", STT(dn[:, h, 0:1], self.ps[tbk][:, 256:257], -1.0, dn[:, h, 2:3], ALU.mult, ALU.max),
                     reads=[self.psres[tbk], dnr[h]], writes=[dnr[h]])
                S.op("dve", RCP(dn[:, h, 1:2], dn[:, h, 0:1]), reads=[dnr[h]], writes=[dnr[h]])
                S.op("act", ACT(H[s][:, h * 256:(h + 1) * 256], self.ps[tbk][:, 0:256], AF.Copy,
                                scale=dn[:, h, 1:2]), reads=[self.psres[tbk], dnr[h]], writes=[R["H"][s]])
                dbk = 4 + (dc_i % 2)
                dc_i += 1
                S.mm([MM(self.ps[dbk][:, 0:257], kh[s][:, h, :], VA[s][:, h, 0:257], True, True)],
                     reads=[R["kh"][s], R["VA"][s]], writes=[self.psres[dbk]])
                S.op("dve", STT(C[:, h, 0:257], C[:, h, 0:257], self.EBT[:, d, c, h:h + 1], self.ps[dbk][:, 0:257],
                                ALU.mult, ALU.add), reads=[Cr[h], self.gres, self.psres[dbk]], writes=[Cr[h]])
                S.op("act", ACT(Cb[:, h, 0:257], C[:, h, 0:257], AF.Copy), reads=[Cr[h]], writes=[Cbr[h]])
            if d == 0:
                S.dma("act", R["H"][s], [DMA(X["hf"][rows, :], H[s][:])], reads=[R["H"][s]],
                      writes=[self.Xres["hf"]])
            else:
                S.op("dve", TT(H[s][:], H[s][:], HF[s][:], ALU.add), reads=[HFr[s]], writes=[R["H"][s]])
                for h in range(8):
                    S.op("act", ACT(SQ[:, h * 256:(h + 1) * 256], H[s][:, h * 256:(h + 1) * 256], AF.Square,
                                    accum_out=rs[:, c, h, 0:1]), reads=[R["H"][s]], writes=[SQr, rsr])
                S.op("act", ACT(rs[:, c, :, 1], rs[:, c, :, 0], AF.Sqrt, bias=EPS, scale=1.0 / 256.0), reads=[rsr],
                     writes=[rsr])
                S.op("dve", RCP(rs[:, c, :, 2], rs[:, c, :, 1]), reads=[rsr], writes=[rsr])
                S.op("dve", TT(HN[:], OG[s][:], MLN[:], ALU.mult), reads=[OGr[s], MLNr], writes=[HNr])
                for h in range(8):
                    S.op("dve", STT(OB[s][:, h * 256:(h + 1) * 256], H[s][:, h * 256:(h + 1) * 256],
                                    rs[:, c, h, 2:3], HN[:, h * 256:(h + 1) * 256], ALU.mult, ALU.mult),
                         reads=[R["H"][s], rsr, HNr], writes=[OBr[s]])
                self.transposes_to(OB[s], OBr[s], OT[s], OTr[s])
                S.dma("act", OTr[s], [DMA(X["oT"][c], OT[s][:])], reads=[OTr[s]], writes=[self.Xres["oT"]])
        self.stage_end(st, "scan%d_%d" % (l, d))

    def layer_na(self, l, need_ctx):
        self.na_qkv(l, need_ctx)
        self.check("qkv%d" % l)
        self.na_att(l, need_ctx)

    def na_qkv(self, l, need_ctx):
        S, X = self.S, self.X
        j = l // 2
        st = self.stage_begin()
        self.ensure_casts("na_w_qkv", l)
        Wap, Wres = self.W["na_w_qkv"][j], self.Wres[("na_w_qkv", l)]
        g = self.gemm_setup(st)
        sbf = [self.sb(st, "qsb%d" % i, [128, 512], BF16) for i in range(4)]
        sbr = [Res() for _ in range(4)]
        ci = 0
        scale = 128.0 ** -0.5
        for (t0, nt) in self.blocks(True):
            ut = self.load_ut(g, X["uT"], self.Xres["uT"], t0, nt)
            n = nt * 128
            tok0 = t0 * 128
            for cb in range(8):
                if cb < 4 and t0 < 2 and not need_ctx:
                    continue

                def epi(o, ps, psr, cb=cb, n=n, tok0=tok0):
                    nonlocal ci
                    s = ci % 4
                    ci += 1
                    hd = (cb % 4) * 4 + o
                    if cb < 4:
                        S.op("act", ACT(sbf[s][:, 0:n], ps[:, 0:n], AF.Copy, scale=scale), reads=[psr],
                             writes=[sbr[s]])
                        S.dma("pool", sbr[s], [DMA(X["qT"][hd, :, tok0:tok0 + n], sbf[s][:, 0:n])], reads=[sbr[s]],
                              writes=[self.Xres["qT"]])
                    else:
                        S.op("dve", CP(sbf[s][:, 0:n], ps[:, 0:n]), reads=[psr], writes=[sbr[s]])
                        S.dma("pool", sbr[s], [DMA(X["kT"][hd, :, tok0:tok0 + n], sbf[s][:, 0:n])], reads=[sbr[s]],
                              writes=[self.Xres["kT"]])

                self.gemm_cols(g, "fm", [ut], nt, Wap, Wres, cb * 512, 512, epi)
            for cb in range(4):
                def epiv(o, ps, psr, cb=cb, t0=t0):
                    nonlocal ci
                    s = ci % 4
                    ci += 1
                    t = t0 + o
                    S.op("act" if (ci % 2) else "dve",
                         ACT(sbf[s][:], ps[:, :], AF.Copy) if (ci % 2) else CP(sbf[s][:], ps[:, :]), reads=[psr],
                         writes=[sbr[s]])
                    S.dma("pool", sbr[s], [DMA(X["v_tm"][t * 128:(t + 1) * 128, cb * 512:(cb + 1) * 512], sbf[s][:])],
                          reads=[sbr[s]], writes=[self.Xres["v_tm"]])

                self.gemm_cols(g, "tm", [ut], nt, Wap, Wres, 4096 + cb * 512, 512, epiv)
        self.stage_end(st, "qkv%d" % l)

    def na_att(self, l, need_ctx):
        S, X, I = self.S, self.X, self.I
        j = l // 2
        st = self.stage_begin()
        KT = [self.sb(st, "aKT%d" % i, [128, NTOK], BF16) for i in range(2)]
        QT = [self.sb(st, "aQT%d" % i, [128, NTOK], BF16) for i in range(2)]
        V = [self.sb(st, "aV%d" % i, [128, NT, 128], BF16) for i in range(2)]
        BI = [self.sb(st, "aBI%d" % i, [128, 21, 128], F32) for i in range(2)]
        OTh = [self.sb(st, "aOT%d" % i, [128, NT, 128], BF16) for i in range(2)]
        R = {n: [Res() for _ in range(2)] for n in ("KT", "QT", "V", "BI", "OT")}
        E = [self.sb(st, "aE%d" % i, [128, 5, 128], F32) for i in range(2)]
        Er = [Res() for _ in range(2)]
        PT = [self.sb(st, "aPT%d" % i, [128, 7, 128], BF16) for i in range(3)]
        PTr = [Res() for _ in range(3)]
        rd = [self.sb(st, "ard%d" % i, [128, 128], F32) for i in range(2)]
        rdr = [Res() for _ in range(2)]
        rbias = Res()
        qtiles = ([0, 1] if need_ctx else []) + list(range(2, NT))
        items = [(h, tq) for h in range(16) for tq in qtiles]

        def load_head(h):
            s = h % 2
            self.pace_casts(8)
            S.dma("sp", R["KT"][s], [DMA(KT[s][:], X["kT"][h])], reads=[self.Xres["kT"]], writes=[R["KT"][s]])
            S.dma("sp", R["QT"][s], [DMA(QT[s][:], X["qT"][h])], reads=[self.Xres["qT"]], writes=[R["QT"][s]])
            S.dma("sp", R["V"][s],
                  [DMA(V[s][:], X["v_tm"][:, h * 128:(h + 1) * 128].rearrange("(t p) e -> p t e", p=128))],
                  reads=[self.Xres["v_tm"]], writes=[R["V"][s]])
            S.dma("sp", R["BI"][s], [DMA(BI[s][:], I["na_bias"][j, h].rearrange("p (t q) -> p t q", q=128))],
                  reads=[rbias], writes=[R["BI"][s]])

        def cfg(tq):
            if tq < 2:
                return [], 0
            jq = tq - 2
            if 2 <= jq <= 29:
                return list(range(jq - 2, jq + 3)), 0
            if jq == 0:
                return [0, 1, 2, 3], 5
            if jq == 1:
                return [0, 1, 2, 3], 9
            if jq == 30:
                return [28, 29, 30, 31], 13
            return [28, 29, 30, 31], 17

        def phase_a(qi, h, tq):
            s = h % 2
            band, btile0 = cfg(tq)
            nb = len(band)
            x = qi % 2
            bA, bB = 2 * x, 2 * x + 1
            qs = slice(tq * 128, (tq + 1) * 128)
            chunks = [2 + m for m in band] + [0, 1]
            fA, fB = [], []
            for i, kt in enumerate(chunks):
                ks = slice(kt * 128, (kt + 1) * 128)
                if i < nb and i < 4:
                    fA.append(MM(self.ps[bA][:, i * 128:(i + 1) * 128], KT[s][:, ks], QT[s][:, qs], True, True))
                else:
                    slot = (i - 4) if i < nb else (1 + i - nb)
                    fB.append(MM(self.ps[bB][:, slot * 128:(slot + 1) * 128], KT[s][:, ks], QT[s][:, qs], True,
                                 True))
            if fA:
                S.mm(fA, reads=[R["KT"][s], R["QT"][s]], writes=[self.psres[bA]])
            S.mm(fB, reads=[R["KT"][s], R["QT"][s]], writes=[self.psres[bB]])

        def phase_b(qi, h, tq):
            s = h % 2
            band, btile0 = cfg(tq)
            nb = len(band)
            x = qi % 2
            p3 = qi % 3
            bA, bB, bO = 2 * x, 2 * x + 1, 4 + x
            chunks = [2 + m for m in band] + [0, 1]
            if nb:
                n4 = min(nb, 4)
                S.op("dve", TT(E[x][:, 0:n4, :], self.ps[bA][:, 0:n4 * 128].rearrange("p (a b) -> p a b", b=128),
                               BI[s][:, btile0:btile0 + n4, :], ALU.add), reads=[self.psres[bA], R["BI"][s]],
                     writes=[Er[x]])
                if nb == 5:
                    S.op("dve", TT(E[x][:, 4, :], self.ps[bB][:, 0:128], BI[s][:, btile0 + 4, :], ALU.add),
                         reads=[self.psres[bB], R["BI"][s]], writes=[Er[x]])
                S.op("act", ACT(PT[p3][:, 0:nb, :], E[x][:, 0:nb, :], AF.Exp), reads=[Er[x]], writes=[PTr[p3]])
            S.op("act", ACT(PT[p3][:, 5:7, :], self.ps[bB][:, 128:384].rearrange("p (a b) -> p a b", b=128),
                            AF.Exp), reads=[self.psres[bB]], writes=[PTr[p3]])

        def phase_c(qi, h, tq):
            s = h % 2
            band, btile0 = cfg(tq)
            nb = len(band)
            x = qi % 2
            p3 = qi % 3
            bO = 4 + x
            chunks = [2 + m for m in band] + [0, 1]
            fns = []
            nchunk = len(chunks)
            for i, kt in enumerate(chunks):
                pslot = i if i < nb else 5 + (i - nb)
                fns.append(MM(self.ps[bO][:, 0:128], V[s][:, kt, :], PT[p3][:, pslot, :], i == 0, i == nchunk - 1))
            for i, kt in enumerate(chunks):
                pslot = i if i < nb else 5 + (i - nb)
                fns.append(MM(self.ps[bO][:, 128:256], self.ones_bf, PT[p3][:, pslot, :], i == 0, i == nchunk - 1))
            S.mm(fns, reads=[R["V"][s], PTr[p3], self.cres], writes=[self.psres[bO]])
            S.op("dve", RCP(rd[x][:], self.ps[bO][:, 128:256]), reads=[self.psres[bO]], writes=[rdr[x]])
            S.op("dve", TT(OTh[s][:, tq, :], self.ps[bO][:, 0:128], rd[x][:], ALU.mult),
                 reads=[self.psres[bO], rdr[x]], writes=[R["OT"][s]])
            if tq == qtiles[-1]:
                t_lo = 0 if need_ctx else 2
                S.dma("act", R["OT"][s],
                      [DMA(X["oT"][t_lo:NT, :, h, :].rearrange("t p c -> p t c"), OTh[s][:, t_lo:NT, :])],
                      reads=[R["OT"][s]], writes=[self.Xres["oT"]])

        nit = len(items)
        for qi in range(nit + 2):
            if qi < nit:
                h, tq = items[qi]
                if tq == qtiles[0] and h == 0:
                    load_head(0)
                phase_a(qi, h, tq)
            if 0 <= qi - 1 < nit:
                phase_b(qi - 1, *items[qi - 1])
            if 0 <= qi - 2 < nit:
                phase_c(qi - 2, *items[qi - 2])
                ph, ptq = items[qi - 2]
                if ptq == qtiles[-1] and ph + 2 < 16:
                    load_head(ph + 2)
            if qi == 0 and 1 < 16:
                load_head(1)
        self.stage_end(st, "att%d" % l)


def _consts():
    cm = np.zeros((128, 512), np.float32)
    cm[:, 0:128] = np.eye(128, dtype=np.float32)
    cm[:, 128:256] = 1.0
    jj = np.arange(128)[:, None]
    ss = np.arange(128)[None, :]
    cm[:, 256:384] = (jj <= ss).astype(np.float32)
    cm[:, 384:512] = (jj >= ss).astype(np.float32)
    pos = np.arange(T)
    row = (pos // 64).astype(np.float32)
    col = (pos % 64).astype(np.float32)
    nf = 32
    inv = (np.float32(10000.0) ** (-np.arange(nf, dtype=np.float32) / np.float32(nf))).astype(np.float32)
    ang_r = row[:, None] * inv[None, :]
    ang_c = col[:, None] * inv[None, :]
    cos128 = np.concatenate([np.cos(ang_r), np.cos(ang_r), np.cos(ang_c), np.cos(ang_c)], axis=1)
    sin128 = np.concatenate([-np.sin(ang_r), np.sin(ang_r), -np.sin(ang_c), np.sin(ang_c)], axis=1)
    rope = np.stack([np.tile(cos128, (1, 4)), np.tile(sin128, (1, 4))], axis=1).astype(np.float32)
    return cm, rope


def _bias_index():
    cfgs = [(jq0, list(range(jq0 - 2, jq0 + 3))) for jq0 in [2]] + [(0, [0, 1, 2, 3]), (1, [0, 1, 2, 3]),
                                                                    (30, [28, 29, 30, 31]), (31, [28, 29, 30, 31])]
    ri = np.zeros((128, 21, 128), np.int64)
    cidx = np.zeros((128, 21, 128), np.int64)
    valid = np.zeros((128, 21, 128), bool)
    j = np.arange(128)[:, None]
    s = np.arange(128)[None, :]
    ti = 0
    for jq, band in cfgs:
        qr = 2 * jq + s // 64
        qc = s % 64
        rs = np.clip(qr - 4, 0, 56)
        cs = np.clip(qc - 8, 0, 48)
        for m in band:
            kr = 2 * m + j // 64
            kc = j % 64
            ok = (kr >= rs) & (kr < rs + 8) & (kc >= cs) & (kc < cs + 16)
            ro = np.clip(kr - qr + 7, 0, 14)
            co = np.clip(kc - qc + 15, 0, 30)
            ri[:, ti, :] = np.broadcast_to(ro, (128, 128))
            cidx[:, ti, :] = np.broadcast_to(co, (128, 128))
            valid[:, ti, :] = ok
            ti += 1
    assert ti == 21
    return ri, cidx, valid


_CACHE = {}


def _program():
    if "nc" not in _CACHE:
        p = Prog()
        _CACHE["nc"] = p.build()
    return _CACHE["nc"]


def _host_inputs(x, c, ctx, c_ctx, ada_w, ada_b, norm_mix, norm_mlp, ml_w_in, ml_b_gate, ml_norm, ml_w_out,
                 na_w_qkv, na_rpb, na_w_out, mlp_w1, mlp_w2, final_norm, ncores=8):
    f = lambda a: np.ascontiguousarray(np.asarray(a, dtype=np.float32))
    cm, rope = _consts()
    ri, cidx, valid = _bias_index()
    rpb = f(na_rpb)
    nb = rpb[:, :, ri, cidx]
    nb = np.where(valid[None, None], nb, np.float32(NEG)).astype(np.float32)
    nb = np.ascontiguousarray(nb.reshape(2, 16, 128, 21 * 128))
    shared = {
        "ada_w": f(ada_w), "ada_b": f(ada_b), "norm_mix": f(norm_mix), "norm_mlp": f(norm_mlp),
        "ml_w_in": f(ml_w_in), "ml_b_gate": f(ml_b_gate), "ml_norm": f(ml_norm), "ml_w_out": f(ml_w_out),
        "na_w_qkv": f(na_w_qkv), "na_bias": nb, "na_w_out": f(na_w_out), "mlp_w1": f(mlp_w1), "mlp_w2": f(mlp_w2),
        "final_norm": f(final_norm).reshape(1, D), "cmat": cm, "rope": rope,
    }
    cc = f(c_ctx).reshape(16, 128).T
    maps = []
    for b in range(ncores):
        cond = np.concatenate([f(c[b]).reshape(16, 128).T, cc], axis=1)
        m = dict(shared)
        m["x"] = f(x[b])
        m["ctx"] = f(ctx[b])
        m["cond"] = np.ascontiguousarray(cond)
        maps.append(m)
    return maps


def kernel(**inputs):
    nc = _program()
    maps = _host_inputs(**inputs)
    res = run_bass_kernel_spmd(nc, maps, core_ids=list(range(8)))
    return np.stack([np.asarray(r["y"], dtype=np.float32) for r in res.results], axis=0)
```

```python
import math
from contextlib import ExitStack
import numpy as np
import concourse.bass as bass
import concourse.mybir as mybir
from concourse.bass_utils import run_bass_kernel_spmd

F32 = mybir.dt.float32
BF16 = mybir.dt.bfloat16
AF = mybir.ActivationFunctionType
ALU = mybir.AluOpType
AX = mybir.AxisListType

D = 2048
T = 4096
LC = 256
NT = (T + LC) // 128
NTOK = T + LC
DEPTH = 4
DFF = 8192
ML_IN = 6176
EPS = 1e-6
NEG = -30000.0
PROFILE_SCOPES = False


class Res:
    __slots__ = ("w", "r", "chan")

    def __init__(self):
        self.w = {}
        self.r = {}
        self.chan = None


class Sched:
    ENGS = ("pe", "act", "dve", "pool", "sp")

    def __init__(self, nc, es, nchan=40):
        self.nc = nc
        self.semobj = {}
        self.cnt = {}
        for e in ("pe", "act", "dve", "pool"):
            self.semobj[e] = es.enter_context(nc.semaphore("sem_" + e))
            self.cnt[e] = 0
        self.chans = []
        for i in range(nchan):
            k = "ch%d" % i
            self.semobj[k] = es.enter_context(nc.semaphore("sem_" + k))
            self.cnt[k] = 0
            self.chans.append(k)
        self.next_chan = 0
        self.q = {e: [] for e in self.ENGS}
        self.waited = {e: {} for e in self.ENGS}
        self.ninstr = 0

    def new_stage(self):
        self.next_chan = 8

    def _chan(self, res):
        if res.chan is None or res.chan[1] != id(self.q):
            pass
        if res.chan is None:
            assert self.next_chan < len(self.chans), "out of dma channels"
            res.chan = self.chans[self.next_chan]
            self.next_chan += 1
        return res.chan

    def _waits(self, eng, reads, writes, extra=()):
        best = {}
        wd = self.waited[eng]
        for r in reads:
            for k, v in r.w.items():
                if v > best.get(k, 0):
                    best[k] = v
        for w in writes:
            for k, v in w.w.items():
                if v > best.get(k, 0):
                    best[k] = v
            for k, v in w.r.items():
                if v > best.get(k, 0):
                    best[k] = v
        for k, v in extra:
            if v > best.get(k, 0):
                best[k] = v
        out = []
        for k, v in best.items():
            if wd.get(k, 0) < v:
                wd[k] = v
                out.append((k, v))
        return out

    def _commit(self, tok, reads, writes):
        k, v = tok
        for r in reads:
            if r.r.get(k, 0) < v:
                r.r[k] = v
        for w in writes:
            if w.w.get(k, 0) < v:
                w.w[k] = v

    def op(self, eng, fn, reads=(), writes=()):
        waits = self._waits(eng, reads, writes)
        self.cnt[eng] += 1
        tok = (eng, self.cnt[eng])
        if eng != "pe":
            self.waited[eng][eng] = max(self.waited[eng].get(eng, 0), 0)
        self.q[eng].append((waits, fn, (eng, 1)))
        self._commit(tok, reads, writes)
        self.ninstr += 1
        return tok

    def mm(self, fns, reads=(), writes=()):
        waits = [w for w in self._waits("pe", reads, writes) if w[0] != "pe"]
        self.cnt["pe"] += 1
        tok = ("pe", self.cnt["pe"])
        n = len(fns)
        for i, fn in enumerate(fns):
            self.q["pe"].append((waits if i == 0 else (), fn, ("pe", 1) if i == n - 1 else None))
        self._commit(tok, reads, writes)
        self.ninstr += n
        return tok

    def dma(self, queue, owner, fns, reads=(), writes=()):
        ch = self._chan(owner)
        waits = self._waits(queue, reads, writes, extra=((ch, self.cnt[ch]),))
        self.cnt[ch] += 16 * len(fns)
        tok = (ch, self.cnt[ch])
        for i, fn in enumerate(fns):
            self.q[queue].append((waits if i == 0 else (), fn, (ch, 16)))
        self._commit(tok, reads, writes)
        self.ninstr += len(fns)
        return tok

    def barrier(self):
        reserved = set(self.chans[0:8])
        allk = [(k, v) for k, v in self.cnt.items() if v > 0 and k not in reserved]
        for e in self.ENGS:
            wd = self.waited[e]
            ws = []
            for k, v in allk:
                if wd.get(k, 0) < v:
                    wd[k] = v
                    ws.append((k, v))
            if ws:
                self.q[e].append((ws, None, None))

    def flush(self, name=None):
        nc = self.nc
        if not any(self.q[e] for e in self.ENGS):
            return
        with ExitStack() as _es:
            if name is not None and PROFILE_SCOPES:
                _es.enter_context(nc.named_scope(name))
            block = _es.enter_context(nc.Block())
            decos = {"pe": block.tensor, "act": block.scalar, "dve": block.vector, "pool": block.gpsimd,
                     "sp": block.sync}
            for e in self.ENGS:
                items = self.q[e]
                if not items:
                    continue
                semobj = self.semobj

                def body(eng, items=items):
                    for waits, fn, inc in items:
                        for k, v in waits:
                            eng.wait_ge(semobj[k], v)
                        if fn is None:
                            continue
                        ins = fn(eng)
                        if inc is not None:
                            ins.then_inc(semobj[inc[0]], inc[1])

                decos[e](body)
                self.q[e] = []


def TT(out, in0, in1, op):
    return lambda e: e.tensor_tensor(out=out, in0=in0, in1=in1, op=op)


def STT(out, in0, scalar, in1, op0, op1):
    return lambda e: e.scalar_tensor_tensor(out=out, in0=in0, scalar=scalar, in1=in1, op0=op0, op1=op1)


def TS(out, in0, s1, s2, op0, op1=None):
    if op1 is None:
        return lambda e: e.tensor_scalar(out=out, in0=in0, scalar1=s1, scalar2=None, op0=op0)
    return lambda e: e.tensor_scalar(out=out, in0=in0, scalar1=s1, scalar2=s2, op0=op0, op1=op1)


def ACT(out, in_, func, bias=None, scale=None, accum_out=None):
    kw = {}
    if bias is not None:
        kw["bias"] = bias
    if scale is not None:
        kw["scale"] = scale
    if accum_out is not None:
        kw["accum_out"] = accum_out
    return lambda e: e.activation(out=out, in_=in_, func=func, **kw)


def CP(out, in_):
    return lambda e: e.tensor_copy(out=out, in_=in_)


def RCP(out, in_):
    return lambda e: e.reciprocal(out=out, in_=in_)


def MM(out, lhsT, rhs, start, stop):
    return lambda e: e.matmul(out, lhsT=lhsT, rhs=rhs, start=start, stop=stop)


def TR(out, in_, ident):
    return lambda e: e.transpose(out, in_, ident)


def DMA(out, in_):
    return lambda e: e.dma_start(out=out, in_=in_)


def MEMSET(ap, v):
    return lambda e: e.memset(ap, v)


def RED(out, in_, op=None):
    return lambda e: e.tensor_reduce(out=out, in_=in_, axis=AX.X, op=ALU.add if op is None else op)


class _Stop(Exception):
    pass


class Prog:
    def check(self, name):
        if self.stop_after == name:
            raise _Stop()

    def __init__(self, stop_after=None, debug=()):
        self.stop_after = stop_after
        self.debug = debug
        self.nc = bass.Bass("TRN2", target_bir_lowering=False)
        self.es = ExitStack()
        self.stages_done = []

    def din(self, name, shape, dt=F32):
        return self.nc.dram_tensor(name, list(shape), dt, kind="ExternalInput").ap()

    def dscr(self, name, shape, dt=F32):
        return self.nc.dram_tensor(name, list(shape), dt, kind="Internal").ap()

    def dout(self, name, shape, dt=F32):
        return self.nc.dram_tensor(name, list(shape), dt, kind="ExternalOutput").ap()

    def sb(self, st, name, shape, dt):
        self._uid = getattr(self, "_uid", 0) + 1
        return st.enter_context(self.nc.sbuf_tensor("%s_u%d" % (name, self._uid), list(shape), dt))

    def stage_begin(self):
        self.S.new_stage()
        self.S.barrier()
        return ExitStack()

    def stage_end(self, st, name):
        self.S.barrier()
        self.S.flush(name)
        st.close()
        self.stages_done.append(name)

    def build(self):
        nc = self.nc
        es = self.es
        I = {}
        I["x"] = self.din("x", [T, D])
        I["ctx"] = self.din("ctx", [LC, D])
        I["cond"] = self.din("cond", [128, 32])
        I["ada_w"] = self.din("ada_w", [DEPTH, D, 6 * D])
        I["ada_b"] = self.din("ada_b", [DEPTH, 6 * D])
        I["norm_mix"] = self.din("norm_mix", [DEPTH, D])
        I["norm_mlp"] = self.din("norm_mlp", [DEPTH, D])
        I["ml_w_in"] = self.din("ml_w_in", [2, D, ML_IN])
        I["ml_b_gate"] = self.din("ml_b_gate", [2, 32])
        I["ml_norm"] = self.din("ml_norm", [2, D])
        I["ml_w_out"] = self.din("ml_w_out", [2, D, D])
        I["na_w_qkv"] = self.din("na_w_qkv", [2, D, 3 * D])
        I["na_bias"] = self.din("na_bias", [2, 16, 128, 21 * 128])
        I["na_w_out"] = self.din("na_w_out", [2, D, D])
        I["mlp_w1"] = self.din("mlp_w1", [DEPTH, D, DFF])
        I["mlp_w2"] = self.din("mlp_w2", [DEPTH, DFF, D])
        I["final_norm"] = self.din("final_norm", [1, D])
        I["cmat"] = self.din("cmat", [128, 4 * 128])
        I["rope"] = self.din("rope", [T, 2, 512])
        self.I = I
        W = {}
        W["ml_w_in"] = self.dscr("b_ml_w_in", [2, D, ML_IN], BF16)
        W["ml_w_out"] = self.dscr("b_ml_w_out", [2, D, D], BF16)
        W["na_w_qkv"] = self.dscr("b_na_w_qkv", [2, D, 3 * D], BF16)
        W["na_w_out"] = self.dscr("b_na_w_out", [2, D, D], BF16)
        W["mlp_w1"] = self.dscr("b_mlp_w1", [DEPTH, D, DFF], BF16)
        W["mlp_w2"] = self.dscr("b_mlp_w2", [DEPTH, DFF, D], BF16)
        self.W = W
        self.Wres = {(k, l): Res() for k in W for l in range(4)}
        X = {}
        X["h"] = self.dscr("hbuf", [NTOK, D])
        X["uT"] = self.dscr("uT", [NT, 128, 16, 128], BF16)
        X["oT"] = self.dscr("oT", [NT, 128, 16, 128], BF16)
        X["modv"] = self.dscr("modv", [DEPTH, 2, 6, D])
        X["q_tm"] = self.dscr("q_tm", [NTOK, 1024])
        X["k_tm"] = self.dscr("k_tm", [NTOK, 1024])
        X["v_tm"] = self.dscr("v_tm", [NTOK, D], BF16)
        X["og_tm"] = self.dscr("og_tm", [NTOK, D])
        X["hf"] = self.dscr("hf", [NTOK, D])
        X["qT"] = self.dscr("qT", [16, 128, NTOK], BF16)
        X["kT"] = self.dscr("kT", [16, 128, NTOK], BF16)
        self.X = X
        self.Xres = {k: Res() for k in X}
        self.y = self.dout("y", [T, D])
        self.yres = Res()
        self.dbg = {}
        for name, shape, dt in self.debug:
            self.dbg[name] = self.dout("dbg_" + name, shape, dt)

        self.S = Sched(nc, es, nchan=48)
        self.ps = [es.enter_context(nc.psum_tensor("ps%d" % i, [128, 512], F32)) for i in range(8)]
        self.psres = [Res() for _ in range(8)]
        self.cm_f = es.enter_context(nc.sbuf_tensor("cm_f", [128, 4 * 128], F32))
        self.cm_b = es.enter_context(nc.sbuf_tensor("cm_b", [128, 2 * 128], BF16))
        self.cres = Res()
        self.ident = self.cm_b[:, 0:128]
        self.ones_bf = self.cm_b[:, 128:256]
        self.ones_f = self.cm_f[:, 128:256]
        self.maskf = self.cm_f[:, 256:384]
        self.maskb = self.cm_f[:, 384:512]

        try:
            self.stage_prologue()
            self.check("prologue")
            self.stage_mod()
            self.check("mod")
            for i in range(DEPTH):
                need_ctx = i < DEPTH - 1
                self.stage_norm(i, 0)
                self.check("norm%d" % i)
                if i % 2 == 0:
                    self.layer_mlstm(i)
                else:
                    self.layer_na(i, need_ctx)
                self.check("mix%d" % i)
                self.stage_wout_norm(i, need_ctx)
                self.check("wout%d" % i)
                self.stage_mlp(i, need_ctx)
                self.check("mlp%d" % i)
            self.stage_final()
        except _Stop:
            pass
        return self.finish()

    def finish(self):
        self.stage_debug()
        self.es.close()
        return self.nc

    def stage_debug(self):
        if not self.dbg:
            return
        S = self.S
        st = self.stage_begin()
        r = Res()
        for name, ap in self.dbg.items():
            src = self.X[name] if name in self.X else self.W[name]
            if len(src.shape) > 2:
                if name in self.W:
                    srcv = src[0]
                else:
                    srcv = src
            else:
                srcv = src
            S.dma("sp", r, [DMA(ap, srcv)], reads=[], writes=[r])
        self.stage_end(st, "debug")

    def stage_prologue(self):
        S, I, W, X = self.S, self.I, self.W, self.X
        st = self.stage_begin()
        S.dma("sp", self.cres, [DMA(self.cm_f[:], I["cmat"][:, :])], writes=[self.cres])
        S.op("dve", CP(self.cm_b[:], self.cm_f[:, 0:256]), reads=[self.cres], writes=[self.cres])
        hr = self.Xres["h"]
        r1, r2 = Res(), Res()
        S.dma("sp", r1, [DMA(X["h"][0:LC, :], I["ctx"][:, :])], writes=[hr])
        fns = [DMA(X["h"][LC + i * 512:LC + (i + 1) * 512, :], I["x"][i * 512:(i + 1) * 512, :]) for i in range(8)]
        S.dma("sp", r2, fns, writes=[hr])
        self.cast_res = [Res() for _ in range(8)]
        for i, r in enumerate(self.cast_res):
            r.chan = self.S.chans[i]
        self.cast_fifo = []
        self.cast_i = 0

        def cast(key, l, li):
            src = I[key][li]
            dst = W[key][li]
            rows = src.shape[0]
            step = 128
            for r0 in range(0, rows, step):
                self.cast_fifo.append((key, l, dst[r0:r0 + step, :], src[r0:r0 + step, :]))

        for l in range(DEPTH):
            j = l // 2
            if l % 2 == 0:
                cast("ml_w_in", l, j)
                cast("ml_w_out", l, j)
            else:
                cast("na_w_qkv", l, j)
                cast("na_w_out", l, j)
            cast("mlp_w1", l, l)
            cast("mlp_w2", l, l)
        self.ensure_casts("ml_w_in", 0)
        self.stage_end(st, "prologue")

    def pace_casts(self, n):
        for _ in range(n):
            if not self.cast_fifo:
                return
            key, l, dst, src = self.cast_fifo.pop(0)
            r = self.cast_res[self.cast_i % 8]
            self.cast_i += 1
            self.S.dma("pool", r, [DMA(dst, src)], writes=[self.Wres[(key, l)]])

    def ensure_casts(self, key, l):
        last = -1
        for i, job in enumerate(self.cast_fifo):
            if job[0] == key and job[1] == l:
                last = i
        self.pace_casts(last + 1)

    def stage_mod(self):
        S, I, X = self.S, self.I, self.X
        nc = self.nc
        st = self.stage_begin()
        cin = self.sb(st, "cin", [128, 32], F32)
        cT = self.sb(st, "cT", [128, 16, 2], F32)
        sig = self.sb(st, "csig", [128, 32], F32)
        modrow = self.sb(st, "modrow", [2, 6 * D], F32)
        brow = [self.sb(st, "brow%d" % i, [2, 512], F32) for i in range(2)]
        nrow = self.sb(st, "nrow", [2, 2, D], F32)
        orow = self.sb(st, "orow", [2, 6, D], F32)
        slabs = [self.sb(st, "aslab%d" % i, [128, 8, 512], F32) for i in range(4)]
        sres = [Res() for _ in range(4)]
        rc, rmod, rn, ro = Res(), Res(), Res(), Res()
        rb = [Res(), Res()]
        S.dma("sp", rc, [DMA(cin[:], I["cond"][:, :])], writes=[rc])
        S.op("act", ACT(sig[:], cin[:], AF.Sigmoid), reads=[rc], writes=[rmod])
        S.op("dve", TT(cT[:, :, 0], cin[:, 0:16], sig[:, 0:16], ALU.mult), reads=[rc, rmod], writes=[rc])
        S.op("dve", TT(cT[:, :, 1], cin[:, 16:32], sig[:, 16:32], ALU.mult), reads=[rc, rmod], writes=[rc])
        si = 0
        for l in range(DEPTH):
            S.dma("sp", rn, [DMA(nrow[0:1, 0, :], I["norm_mix"][l:l + 1, :]),
                             DMA(nrow[1:2, 0, :], I["norm_mix"][l:l + 1, :]),
                             DMA(nrow[0:1, 1, :], I["norm_mlp"][l:l + 1, :]),
                             DMA(nrow[1:2, 1, :], I["norm_mlp"][l:l + 1, :])], writes=[rn])
            wv = I["ada_w"][l].rearrange("(k p) n -> p k n", p=128)
            for cb in range(24):
                b = cb % 2
                cs = slice(cb * 512, (cb + 1) * 512)
                S.dma("sp", rb[b], [DMA(brow[b][0:1, :], I["ada_b"][l:l + 1, cs]),
                                    DMA(brow[b][1:2, :], I["ada_b"][l:l + 1, cs])], writes=[rb[b]])
                for kg in range(2):
                    s = si % 4
                    si += 1
                    S.dma("sp", sres[s], [DMA(slabs[s][:], wv[:, kg * 8:(kg + 1) * 8, cs])], writes=[sres[s]])
                    fns = [MM(self.ps[b][0:2, :], cT[:, kg * 8 + k, :], slabs[s][:, k, :], kg == 0 and k == 0,
                              kg == 1 and k == 7) for k in range(8)]
                    S.mm(fns, reads=[rc, sres[s]], writes=[self.psres[b]])
                S.op("dve", TT(modrow[:, cs], self.ps[b][0:2, :], brow[b][:], ALU.add),
                     reads=[self.psres[b], rb[b]], writes=[rmod])
            S.op("dve", STT(orow[:, 0, :], modrow[:, D:2 * D], 1.0, nrow[:, 0, :], ALU.add, ALU.mult),
                 reads=[rmod, rn], writes=[ro])
            S.op("dve", CP(orow[:, 1, :], modrow[:, 0:D]), reads=[rmod], writes=[ro])
            S.op("dve", CP(orow[:, 2, :], modrow[:, 2 * D:3 * D]), reads=[rmod], writes=[ro])
            S.op("dve", STT(orow[:, 3, :], modrow[:, 4 * D:5 * D], 1.0, nrow[:, 1, :], ALU.add, ALU.mult),
                 reads=[rmod, rn], writes=[ro])
            S.op("dve", CP(orow[:, 4, :], modrow[:, 3 * D:4 * D]), reads=[rmod], writes=[ro])
            S.op("dve", CP(orow[:, 5, :], modrow[:, 5 * D:6 * D]), reads=[rmod], writes=[ro])
            S.dma("sp", ro, [DMA(X["modv"][l], orow[:])], reads=[ro], writes=[self.Xres["modv"]])
        self.stage_end(st, "mod")

    def load_bc(self, st, name, vec_ap, res_src, width=D):
        t = self.sb(st, name, [128, width], F32)
        r = Res()
        self.S.dma("sp", r, [DMA(t[:], vec_ap.partition_broadcast(128))], reads=[res_src], writes=[r])
        return t, r

    def modvec(self, l, which, idx):
        return self.X["modv"][l, which, idx:idx + 1, :]

    def transposes_to(self, src_bf, src_res, dst, dst_res, nchunks=16, banks=(6, 7)):
        S = self.S
        for g in range(nchunks // 8):
            b = banks[self._trb % len(banks)]
            self._trb += 1
            pb = self.ps[b][:].bitcast(BF16)
            fns = [TR(pb[:, i * 128:(i + 1) * 128], src_bf[:, (g * 8 + i) * 128:(g * 8 + i + 1) * 128], self.ident)
                   for i in range(8)]
            S.mm(fns, reads=[src_res, self.cres], writes=[self.psres[b]])
            eng = "act" if (self._trb % 2) else "dve"
            if eng == "act":
                S.op("act", ACT(dst[:, g * 8:(g + 1) * 8, :], pb.rearrange("p (a b) -> p a b", b=128), AF.Copy),
                     reads=[self.psres[b]], writes=[dst_res])
            else:
                S.op("dve", CP(dst[:, g * 8:(g + 1) * 8, :], pb.rearrange("p (a b) -> p a b", b=128)),
                     reads=[self.psres[b]], writes=[dst_res])

    _trb = 0

    def stage_norm(self, l, which, need_ctx=True):
        S, X = self.S, self.X
        st = self.stage_begin()
        ai, bi = (0, 1) if which == 0 else (3, 4)
        mres = self.Xres["modv"]
        A = [None, None]
        Bv = [None, None]
        for w in range(2):
            A[w] = self.load_bc(st, "nA%d" % w, self.modvec(l, w, ai), mres)
            Bv[w] = self.load_bc(st, "nB%d" % w, self.modvec(l, w, bi), mres)
        NS = 3
        hs = [self.sb(st, "nh%d" % i, [128, D], F32) for i in range(NS)]
        hres = [Res() for _ in range(NS)]
        junk = self.sb(st, "njunk", [128, D], BF16)
        jres = Res()
        t1 = [self.sb(st, "nt1_%d" % i, [128, D], F32) for i in range(2)]
        t1r = [Res() for _ in range(2)]
        ub = [self.sb(st, "nub%d" % i, [128, D], BF16) for i in range(2)]
        ubr = [Res() for _ in range(2)]
        uTt = [self.sb(st, "nuT%d" % i, [128, 16, 128], BF16) for i in range(2)]
        uTr = [Res() for _ in range(2)]
        stat = self.sb(st, "nstat", [128, NT, 4], F32)
        sres = [Res() for _ in range(NT)]
        S.op("pool", MEMSET(stat[:], 0.0), writes=sres)
        tiles = range(NT) if need_ctx else range(2, NT)
        pend = None
        for n, t in enumerate(tiles):
            w = 0 if t >= 2 else 1
            s = n % NS
            d2 = n % 2
            S.dma("sp", hres[s], [DMA(hs[s][:], X["h"][t * 128:(t + 1) * 128, :])], reads=[self.Xres["h"]],
                  writes=[hres[s]])
            S.op("act", ACT(junk[:], hs[s][:], AF.Square, accum_out=stat[:, t, 0:1]), reads=[hres[s]],
                 writes=[jres, sres[t]])
            S.op("act", ACT(stat[:, t, 1:2], stat[:, t, 0:1], AF.Sqrt, bias=EPS, scale=1.0 / D), reads=[sres[t]],
                 writes=[sres[t]])
            S.op("dve", RCP(stat[:, t, 2:3], stat[:, t, 1:2]), reads=[sres[t]], writes=[sres[t]])
            S.op("dve", STT(t1[d2][:], hs[s][:], stat[:, t, 2:3], A[w][0][:], ALU.mult, ALU.mult),
                 reads=[hres[s], sres[t], A[w][1]], writes=[t1r[d2]])
            S.op("dve", TT(ub[d2][:], t1[d2][:], Bv[w][0][:], ALU.add), reads=[t1r[d2], Bv[w][1]],
                 writes=[ubr[d2]])
            if pend is not None:
                self._norm_part2(*pend)
            pend = (ub[d2], ubr[d2], uTt[d2], uTr[d2], t)
        if pend is not None:
            self._norm_part2(*pend)
        self.stage_end(st, "norm%d_%d" % (l, which))

    def _norm_part2(self, ub, ubr, uTt, uTr, t):
        self.transposes_to(ub, ubr, uTt, uTr)
        self.S.dma("pool", uTr, [DMA(self.X["uT"][t], uTt[:])], reads=[uTr], writes=[self.Xres["uT"]])

    def blocks(self, need_ctx=True):
        bl = [(0, 2)] if need_ctx else []
        bl += [(2 + 4 * i, 4) for i in range(8)]
        return bl

    def gemm_setup(self, st, nslab=4, nut=2):
        g = {}
        g["slab"] = [self.sb(st, "gslab%d" % i, [128, 16, 512], BF16) for i in range(nslab)]
        g["slabr"] = [Res() for _ in range(nslab)]
        g["ut"] = [self.sb(st, "gut%d" % i, [128, 4, D], BF16) for i in range(nut)]
        g["utr"] = [Res() for _ in range(nut)]
        g["si"] = 0
        g["ui"] = 0
        g["bset"] = 0
        return g

    def load_ut(self, g, src, src_res, t0, nt):
        s = g["ui"] % len(g["ut"])
        g["ui"] += 1
        ut, utr = g["ut"][s], g["utr"][s]
        self.S.dma("sp", utr, [DMA(ut[:, 0:nt, :], src[t0:t0 + nt].rearrange("t p k c -> p t (k c)"))],
                   reads=[src_res], writes=[utr])
        return ut, utr

    def gemm_cols(self, g, mode, uts, nt, Wap, Wres, c0, cw, epi, nbank=4):
        S = self.S
        Wv = Wap.rearrange("(k p) n -> p k n", p=128)
        nkg = len(uts)
        bs = g["bset"] % (8 // nbank) if nbank <= 4 else 0
        g["bset"] += 1
        banks = [bs * nbank + i for i in range(nbank)]
        nout = nt if mode == "tm" else cw // 128
        for kg in range(nkg):
            s = g["si"] % len(g["slab"])
            g["si"] += 1
            slab, slr = g["slab"][s], g["slabr"][s]
            S.dma("sp", slr, [DMA(slab[:, :, 0:cw], Wv[:, kg * 16:(kg + 1) * 16, c0:c0 + cw])], reads=[Wres],
                  writes=[slr])
            ut, utr = uts[kg]
            for o in range(nout):
                b = banks[o]
                fns = []
                for k in range(16):
                    first = (kg == 0 and k == 0)
                    last = (kg == nkg - 1 and k == 15)
                    if mode == "tm":
                        fns.append(MM(self.ps[b][:, 0:cw], ut[:, o, k * 128:(k + 1) * 128], slab[:, k, 0:cw], first,
                                      last))
                    else:
                        fns.append(MM(self.ps[b][:, 0:nt * 128], slab[:, k, o * 128:(o + 1) * 128],
                                      ut[:, 0:nt, k * 128:(k + 1) * 128], first, last))
                S.mm(fns, reads=[utr, slr], writes=[self.psres[b]])
        for o in range(nout):
            epi(o, self.ps[banks[o]], self.psres[banks[o]])

    def stage_wout(self, l, need_ctx):
        S, X = self.S, self.X
        st = self.stage_begin()
        key = "ml_w_out" if l % 2 == 0 else "na_w_out"
        self.ensure_casts(key, l)
        Wap, Wres = self.W[key][l // 2], self.Wres[(key, l)]
        g = self.gemm_setup(st)
        G = [self.load_bc(st, "woG%d" % w, self.modvec(l, w, 2), self.Xres["modv"]) for w in range(2)]
        hs = [self.sb(st, "woh%d" % i, [128, D], F32) for i in range(6)]
        hres = [Res() for _ in range(6)]
        tmp = [self.sb(st, "wot%d" % i, [128, 512], F32) for i in range(3)]
        tres = [Res() for _ in range(3)]
        hi = 0
        ti = 0
        hr = self.Xres["h"]
        for (t0, nt) in self.blocks(need_ctx):
            w = 0 if t0 >= 2 else 1
            ut = self.load_ut(g, X["oT"], self.Xres["oT"], t0, nt)
            hmap = {}
            for o in range(nt):
                s = hi % 6
                hi += 1
                t = t0 + o
                S.dma("sp", hres[s], [DMA(hs[s][:], X["h"][t * 128:(t + 1) * 128, :])], reads=[hr],
                      writes=[hres[s]])
                hmap[o] = s
            for cb in range(4):
                def epi(o, ps, psr, cb=cb, hmap=hmap, w=w):
                    nonlocal ti
                    s = hmap[o]
                    x = ti % 3
                    ti += 1
                    sl = slice(cb * 512, (cb + 1) * 512)
                    S.op("dve", TT(tmp[x][:], ps[:, :], G[w][0][:, sl], ALU.mult), reads=[psr, G[w][1]],
                         writes=[tres[x]])
                    S.op("pool", TT(hs[s][:, sl], hs[s][:, sl], tmp[x][:], ALU.add), reads=[tres[x], hres[s]],
                         writes=[hres[s]])

                self.gemm_cols(g, "tm", [ut], nt, Wap, Wres, cb * 512, 512, epi)
            for o in range(nt):
                s = hmap[o]
                t = t0 + o
                S.dma("pool", hres[s], [DMA(X["h"][t * 128:(t + 1) * 128, :], hs[s][:])], reads=[hres[s]],
                      writes=[hr])
        self.stage_end(st, "wout%d" % l)


    def stage_wout_norm(self, l, need_ctx):
        S, X = self.S, self.X
        st = self.stage_begin()
        key = "ml_w_out" if l % 2 == 0 else "na_w_out"
        self.ensure_casts(key, l)
        Wap, Wres = self.W[key][l // 2], self.Wres[(key, l)]
        mres = self.Xres["modv"]
        Wsb = self.sb(st, "woW", [128, 16, D], BF16)
        Wr = Res()
        Wv = Wap.rearrange("(k p) n -> p k n", p=128)
        for q4 in range(4):
            S.dma("sp", Wr, [DMA(Wsb[:, q4 * 4:(q4 + 1) * 4, :], Wv[:, q4 * 4:(q4 + 1) * 4, :])], reads=[Wres],
                  writes=[Wr])
        G = [self.load_bc(st, "woG%d" % w, self.modvec(l, w, 2), mres) for w in range(2)]
        A = [self.load_bc(st, "woA%d" % w, self.modvec(l, w, 3), mres) for w in range(2)]
        Bv = [self.load_bc(st, "woB%d" % w, self.modvec(l, w, 4), mres) for w in range(2)]
        NS = 3
        uts = [self.sb(st, "wout%d" % i, [128, 16, 128], BF16) for i in range(NS)]
        utr = [Res() for _ in range(NS)]
        hs = [self.sb(st, "woh%d" % i, [128, D], F32) for i in range(NS)]
        hres = [Res() for _ in range(NS)]
        tmp = [self.sb(st, "wot%d" % i, [128, 512], F32) for i in range(4)]
        tres = [Res() for _ in range(4)]
        junk = self.sb(st, "wojunk", [128, D], BF16)
        jres = Res()
        t1 = [self.sb(st, "wot1_%d" % i, [128, D], F32) for i in range(2)]
        t1r = [Res() for _ in range(2)]
        ub = [self.sb(st, "woub%d" % i, [128, D], BF16) for i in range(2)]
        ubr = [Res() for _ in range(2)]
        uTt = [self.sb(st, "wouT%d" % i, [128, 16, 128], BF16) for i in range(2)]
        uTr = [Res() for _ in range(2)]
        stat = self.sb(st, "wostat", [128, NT, 4], F32)
        sres = [Res() for _ in range(NT)]
        S.op("pool", MEMSET(stat[:], 0.0), writes=sres)
        hr = self.Xres["h"]
        tiles = range(NT) if need_ctx else range(2, NT)
        ti = 0
        bi = 0
        pend = None
        for n, t in enumerate(tiles):
            w = 0 if t >= 2 else 1
            s = n % NS
            d2 = n % 2
            S.dma("sp", utr[s], [DMA(uts[s][:], X["oT"][t])], reads=[self.Xres["oT"]], writes=[utr[s]])
            S.dma("sp", hres[s], [DMA(hs[s][:], X["h"][t * 128:(t + 1) * 128, :])], reads=[hr], writes=[hres[s]])
            for cb in range(4):
                b = bi % 6
                bi += 1
                sl = slice(cb * 512, (cb + 1) * 512)
                fns = [MM(self.ps[b][:, :], uts[s][:, k, :], Wsb[:, k, sl], k == 0, k == 15) for k in range(16)]
                S.mm(fns, reads=[utr[s], Wr], writes=[self.psres[b]])
                x = ti % 4
                ti += 1
                S.op("dve", TT(tmp[x][:], self.ps[b][:, :], G[w][0][:, sl], ALU.mult),
                     reads=[self.psres[b], G[w][1]], writes=[tres[x]])
                S.op("pool" if cb % 2 == 0 else "dve", TT(hs[s][:, sl], hs[s][:, sl], tmp[x][:], ALU.add),
                     reads=[tres[x], hres[s]], writes=[hres[s]])
            S.dma("pool", hres[s], [DMA(X["h"][t * 128:(t + 1) * 128, :], hs[s][:])], reads=[hres[s]], writes=[hr])
            S.op("act", ACT(junk[:], hs[s][:], AF.Square, accum_out=stat[:, t, 0:1]), reads=[hres[s]],
                 writes=[jres, sres[t]])
            S.op("act", ACT(stat[:, t, 1:2], stat[:, t, 0:1], AF.Sqrt, bias=EPS, scale=1.0 / D), reads=[sres[t]],
                 writes=[sres[t]])
            S.op("dve", RCP(stat[:, t, 2:3], stat[:, t, 1:2]), reads=[sres[t]], writes=[sres[t]])
            S.op("dve", STT(t1[d2][:], hs[s][:], stat[:, t, 2:3], A[w][0][:], ALU.mult, ALU.mult),
                 reads=[hres[s], sres[t], A[w][1]], writes=[t1r[d2]])
            S.op("dve", TT(ub[d2][:], t1[d2][:], Bv[w][0][:], ALU.add), reads=[t1r[d2], Bv[w][1]],
                 writes=[ubr[d2]])
            if pend is not None:
                self._norm_part2(*pend)
            pend = (ub[d2], ubr[d2], uTt[d2], uTr[d2], t)
        if pend is not None:
            self._norm_part2(*pend)
        self.stage_end(st, "woutn%d" % l)

    def stage_mlp(self, l, need_ctx):
        S, X = self.S, self.X
        st = self.stage_begin()
        self.ensure_casts("mlp_w2", l)
        W1, W1r = self.W["mlp_w1"][l], self.Wres[("mlp_w1", l)]
        W2, W2r = self.W["mlp_w2"][l], self.Wres[("mlp_w2", l)]
        g = self.gemm_setup(st, nslab=3, nut=2)
        G = [self.load_bc(st, "mlG%d" % w, self.modvec(l, w, 5), self.Xres["modv"]) for w in range(2)]
        aT = [self.sb(st, "aT%d" % i, [128, 4, D], BF16) for i in range(4)]
        aTr = [Res() for _ in range(4)]
        hs = [self.sb(st, "mlh%d" % i, [128, D], F32) for i in range(4)]
        hres = [Res() for _ in range(4)]
        tmp = [self.sb(st, "mlt%d" % i, [128, 512], F32) for i in range(3)]
        tres = [Res() for _ in range(3)]
        hi = 0
        ti = 0
        hr = self.Xres["h"]
        for (t0, nt) in self.blocks(need_ctx):
            w = 0 if t0 >= 2 else 1
            ut = self.load_ut(g, X["uT"], self.Xres["uT"], t0, nt)
            for cb in range(16):
                def epi_up(o, ps, psr, cb=cb):
                    nonlocal ti
                    x = ti % 3
                    ti += 1
                    fch = cb * 4 + o
                    kg, kk = fch // 16, fch % 16
                    n = nt * 128
                    S.op("act", ACT(tmp[x][:, 0:n], ps[:, 0:n], AF.Relu), reads=[psr], writes=[tres[x]])
                    dst = aT[kg][:, 0:nt, kk * 128:(kk + 1) * 128]
                    src = tmp[x][:, 0:n].rearrange("p (t c) -> p t c", c=128)
                    S.op("dve" if (ti % 2) else "pool", TT(dst, src, src, ALU.mult), reads=[tres[x]],
                         writes=[aTr[kg]])

                self.gemm_cols(g, "fm", [ut], nt, W1, W1r, cb * 512, 512, epi_up)
            hmap = {}
            for o in range(nt):
                s = hi % 4
                hi += 1
                t = t0 + o
                S.dma("sp", hres[s], [DMA(hs[s][:], X["h"][t * 128:(t + 1) * 128, :])], reads=[hr],
                      writes=[hres[s]])
                hmap[o] = s
            for cb in range(4):
                def epi_dn(o, ps, psr, cb=cb, hmap=hmap, w=w):
                    nonlocal ti
                    s = hmap[o]
                    x = ti % 3
                    ti += 1
                    sl = slice(cb * 512, (cb + 1) * 512)
                    S.op("dve", TT(tmp[x][:], ps[:, :], G[w][0][:, sl], ALU.mult), reads=[psr, G[w][1]],
                         writes=[tres[x]])
                    S.op("pool", TT(hs[s][:, sl], hs[s][:, sl], tmp[x][:], ALU.add), reads=[tres[x], hres[s]],
                         writes=[hres[s]])

                self.gemm_cols(g, "tm", [(aT[k], aTr[k]) for k in range(4)], nt, W2, W2r, cb * 512, 512, epi_dn)
            for o in range(nt):
                s = hmap[o]
                t = t0 + o
                S.dma("pool", hres[s], [DMA(X["h"][t * 128:(t + 1) * 128, :], hs[s][:])], reads=[hres[s]],
                      writes=[hr])
        self.stage_end(st, "mlp%d" % l)

    def stage_final(self):
        S, X = self.S, self.X
        st = self.stage_begin()
        rfn = Res()
        Gt, Gr = self.load_bc(st, "fnG", self.I["final_norm"][0:1, :], rfn)
        hs = [self.sb(st, "fh%d" % i, [128, D], F32) for i in range(3)]
        hres = [Res() for _ in range(3)]
        os_ = [self.sb(st, "fo%d" % i, [128, D], F32) for i in range(3)]
        ores = [Res() for _ in range(3)]
        junk = self.sb(st, "fjunk", [128, D], BF16)
        jres = Res()
        stat = self.sb(st, "fstat", [128, NT, 4], F32)
        sres = [Res() for _ in range(NT)]
        S.op("pool", MEMSET(stat[:], 0.0), writes=sres)
        for n, t in enumerate(range(2, NT)):
            s = n % 3
            S.dma("sp", hres[s], [DMA(hs[s][:], X["h"][t * 128:(t + 1) * 128, :])], reads=[self.Xres["h"]],
                  writes=[hres[s]])
            S.op("act", ACT(junk[:], hs[s][:], AF.Square, accum_out=stat[:, t, 0:1]), reads=[hres[s]],
                 writes=[jres, sres[t]])
            S.op("act", ACT(stat[:, t, 1:2], stat[:, t, 0:1], AF.Sqrt, bias=EPS, scale=1.0 / D), reads=[sres[t]],
                 writes=[sres[t]])
            S.op("dve", RCP(stat[:, t, 2:3], stat[:, t, 1:2]), reads=[sres[t]], writes=[sres[t]])
            S.op("dve", STT(os_[s][:], hs[s][:], stat[:, t, 2:3], Gt[:], ALU.mult, ALU.mult),
                 reads=[hres[s], sres[t], Gr], writes=[ores[s]])
            S.dma("pool", ores[s], [DMA(self.y[(t - 2) * 128:(t - 1) * 128, :], os_[s][:])], reads=[ores[s]],
                  writes=[self.yres])
        self.stage_end(st, "final")

    def layer_mlstm(self, l):
        lst = ExitStack()
        j = l // 2
        nc = self.nc
        self.GT = self.sb(lst, "GT", [128, NT, 32], F32)
        self.GTr = Res()
        self.EK = self.sb(lst, "EK", [128, 2, NT, 8], F32)
        self.EMB = self.sb(lst, "EMB", [128, 2, NT, 8], F32)
        self.WK = self.sb(lst, "WK", [128, 2, NT, 8], F32)
        self.EBT = self.sb(lst, "EBT", [128, 2, NT, 8], F32)
        self.gres = Res()
        try:
            self.ml_gin(l, j)
            self.check("gin%d" % l)
            self.ml_gates(l, j)
            self.check("gates%d" % l)
            self.ml_scan(l, j, 0)
            self.check("scanf%d" % l)
            self.ml_scan(l, j, 1)
        finally:
            self.S.barrier()
            self.S.flush()
            lst.close()

    def ml_gin(self, l, j):
        S, X, I = self.S, self.X, self.I
        st = self.stage_begin()
        self.ensure_casts("ml_w_in", l)
        Wap, Wres = self.W["ml_w_in"][j], self.Wres[("ml_w_in", l)]
        g = self.gemm_setup(st)
        rbg = Res()
        bg, bgr = self.load_bc(st, "bgbc", I["ml_b_gate"][j:j + 1, :], rbg, width=32)
        rope = [self.sb(st, "rope%d" % i, [128, 2, 512], F32) for i in range(8)]
        roper = [Res() for _ in range(8)]
        sf = [self.sb(st, "gsf%d" % i, [128, 512], F32) for i in range(4)]
        sfr = [Res() for _ in range(4)]
        ta = [self.sb(st, "gta%d" % i, [128, 512], F32) for i in range(2)]
        tar = [Res() for _ in range(2)]
        tb = [self.sb(st, "gtb%d" % i, [128, 512], F32) for i in range(2)]
        tbr = [Res() for _ in range(2)]
        sbf = [self.sb(st, "gsb%d" % i, [128, 512], BF16) for i in range(3)]
        sbr = [Res() for _ in range(3)]
        cnt = {"f": 0, "b": 0, "r": 0, "t": 0}
        rconst = Res()
        for (t0, nt) in self.blocks(True):
            ut = self.load_ut(g, X["uT"], self.Xres["uT"], t0, nt)
            rmap = {}
            if t0 >= 2:
                for o in range(nt):
                    s = cnt["r"] % 8
                    cnt["r"] += 1
                    t = t0 + o
                    S.dma("sp", roper[s], [DMA(rope[s][:], I["rope"][(t - 2) * 128:(t - 1) * 128, :, :])],
                          reads=[rconst], writes=[roper[s]])
                    rmap[o] = s
            for cb in range(13):
                c0 = cb * 512
                cw = 512 if cb < 12 else 32

                def epi(o, ps, psr, cb=cb, t0=t0, rmap=rmap):
                    t = t0 + o
                    rows = slice(t * 128, (t + 1) * 128)
                    if cb < 4:
                        dst = X["q_tm"] if cb < 2 else X["k_tm"]
                        dres = self.Xres["q_tm"] if cb < 2 else self.Xres["k_tm"]
                        s = cnt["f"] % 4
                        cnt["f"] += 1
                        if t >= 2:
                            rs = rmap[o]
                            x = cnt["t"] % 2
                            cnt["t"] += 1
                            S.op("dve", TT(ta[x][:], ps[:, :], rope[rs][:, 0, :], ALU.mult), reads=[psr, roper[rs]],
                                 writes=[tar[x]])
                            pv = ps[:, :].rearrange("p (g a c) -> p g a c", a=2, c=32)
                            sv = rope[rs][:, 1, :].rearrange("p (g a c) -> p g a c", a=2, c=32)
                            tv = tb[x][:].rearrange("p (g a c) -> p g a c", a=2, c=32)
                            S.op("dve", TT(tv[:, :, 0, :], pv[:, :, 1, :], sv[:, :, 0, :], ALU.mult),
                                 reads=[psr, roper[rs]], writes=[tbr[x]])
                            S.op("dve", TT(tv[:, :, 1, :], pv[:, :, 0, :], sv[:, :, 1, :], ALU.mult),
                                 reads=[psr, roper[rs]], writes=[tbr[x]])
                            S.op("pool", TT(sf[s][:], ta[x][:], tb[x][:], ALU.add), reads=[tar[x], tbr[x]],
                                 writes=[sfr[s]])
                        else:
                            S.op("act", ACT(sf[s][:], ps[:, :], AF.Copy), reads=[psr], writes=[sfr[s]])
                        cc = (cb % 2) * 512
                        S.dma("pool", sfr[s], [DMA(dst[rows, cc:cc + 512], sf[s][:])], reads=[sfr[s]],
                              writes=[dres])
                    elif cb < 8:
                        s = cnt["b"] % 3
                        cnt["b"] += 1
                        S.op("act", ACT(sbf[s][:], ps[:, :], AF.Copy), reads=[psr], writes=[sbr[s]])
                        cc = (cb - 4) * 512
                        S.dma("pool", sbr[s], [DMA(X["v_tm"][rows, cc:cc + 512], sbf[s][:])], reads=[sbr[s]],
                              writes=[self.Xres["v_tm"]])
                    elif cb < 12:
                        s = cnt["f"] % 4
                        cnt["f"] += 1
                        S.op("act", ACT(sf[s][:], ps[:, :], AF.Sigmoid), reads=[psr], writes=[sfr[s]])
                        cc = (cb - 8) * 512
                        S.dma("pool", sfr[s], [DMA(X["og_tm"][rows, cc:cc + 512], sf[s][:])], reads=[sfr[s]],
                              writes=[self.Xres["og_tm"]])
                    else:
                        S.op("dve", TT(self.GT[:, t, :], ps[:, 0:32], bg[:], ALU.add), reads=[psr, bgr],
                             writes=[self.GTr])

                self.gemm_cols(g, "tm", [ut], nt, Wap, Wres, c0, cw, epi)
        self.stage_end(st, "gin%d" % l)

    def ml_gates(self, l, j):
        S = self.S
        st = self.stage_begin()
        TH = self.sb(st, "TH", [128, NT, 32], F32)
        IG = self.sb(st, "IG", [128, 2, NT, 8], F32)
        E1 = self.sb(st, "E1", [128, 2, NT, 8], F32)
        LF = self.sb(st, "LF", [128, 2, NT, 8], F32)
        Bs = self.sb(st, "Bs", [128, 2, NT, 8], F32)
        BTs = self.sb(st, "BTs", [128, 2, NT, 8], F32)
        t1 = self.sb(st, "gt1", [128, 2, NT, 8], F32)
        t2 = self.sb(st, "gt2", [128, 2, NT, 8], F32)
        r = Res()
        lns = math.log(128.0 ** -0.5)
        lnsb = self.sb(st, "lnsb", [128, 1], F32)
        S.op("dve", MEMSET(lnsb[:], lns), writes=[r])
        S.op("act", ACT(TH[:], self.GT[:], AF.Tanh, scale=1.0 / 15.0), reads=[self.GTr], writes=[r])
        THv = TH[:].rearrange("p t (d a h) -> p t d a h", d=2, a=2)
        for d in range(2):
            S.op("dve", TS(IG[:, d], THv[:, :, d, 0, :], 15.0, None, ALU.mult), reads=[r], writes=[r])
            S.op("act", ACT(E1[:, d], THv[:, :, d, 1, :], AF.Exp, scale=-15.0), reads=[r], writes=[r])
        S.op("act", ACT(E1[:], E1[:], AF.Ln, bias=1.0), reads=[r], writes=[r])
        S.op("dve", TS(LF[:], E1[:], -1.0, None, ALU.mult), reads=[r], writes=[r])
        n = NT * 8
        for d in range(2):
            lfv = LF[:, d].rearrange("p t h -> p (t h)")
            tri = self.maskf if d == 0 else self.maskb
            S.mm([MM(self.ps[d][:, 0:n], tri, lfv, True, True)], reads=[r, self.cres], writes=[self.psres[d]])
            S.mm([MM(self.ps[2 + d][:, 0:n], self.ones_f, lfv, True, True)], reads=[r, self.cres],
                 writes=[self.psres[2 + d]])
            S.op("dve", CP(Bs[:, d].rearrange("p t h -> p (t h)"), self.ps[d][:, 0:n]), reads=[self.psres[d]],
                 writes=[r])
            S.op("dve", CP(BTs[:, d].rearrange("p t h -> p (t h)"), self.ps[2 + d][:, 0:n]),
                 reads=[self.psres[2 + d]], writes=[r])
        S.op("dve", TT(t1[:], IG[:], Bs[:], ALU.subtract), reads=[r], writes=[r])
        S.op("dve", TT(t2[:], t1[:], BTs[:], ALU.add), reads=[r], writes=[r])
        S.op("act", ACT(self.EK[:], t1[:], AF.Exp, bias=lnsb[:]), reads=[r], writes=[self.gres])
        S.op("act", ACT(self.WK[:], t2[:], AF.Exp, bias=lnsb[:]), reads=[r], writes=[self.gres])
        S.op("act", ACT(self.EMB[:], Bs[:], AF.Exp, scale=-1.0), reads=[r], writes=[self.gres])
        S.op("act", ACT(self.EBT[:], BTs[:], AF.Exp), reads=[r], writes=[self.gres])
        self.stage_end(st, "gates%d" % l)

    def ml_scan(self, l, j, d):
        S, X, I = self.S, self.X, self.I
        st = self.stage_begin()
        NSL = 2
        q32 = [self.sb(st, "q32_%d" % i, [128, 1024], F32) for i in range(NSL)]
        k32 = [self.sb(st, "k32_%d" % i, [128, 1024], F32) for i in range(NSL)]
        qb = [self.sb(st, "qb_%d" % i, [128, 1024], BF16) for i in range(NSL)]
        kb = [self.sb(st, "kb_%d" % i, [128, 1024], BF16) for i in range(NSL)]
        kh = [self.sb(st, "kh_%d" % i, [128, 8, 128], BF16) for i in range(NSL)]
        VA = [self.sb(st, "VA_%d" % i, [128, 8, 264], BF16) for i in range(NSL)]
        QT = [self.sb(st, "QT_%d" % i, [128, 8, 128], BF16) for i in range(NSL)]
        KT = [self.sb(st, "KT_%d" % i, [128, 8, 128], BF16) for i in range(NSL)]
        PT = [self.sb(st, "PT_%d" % i, [128, 8, 128], BF16) for i in range(NSL)]
        H = [self.sb(st, "H_%d" % i, [128, D], F32) for i in range(NSL)]
        R = {n: [Res() for _ in range(NSL)] for n in ("q32", "k32", "qb", "kb", "kh", "VA", "QT", "KT", "H")}
        PTr = [[Res() for _ in range(8)] for _ in range(NSL)]
        C = self.sb(st, "Cst", [128, 8, 264], F32)
        Cb = self.sb(st, "Cbf", [128, 8, 264], BF16)
        Cr = [Res() for _ in range(8)]
        Cbr = [Res() for _ in range(8)]
        dn = self.sb(st, "dn", [128, 8, 4], F32)
        dnr = [Res() for _ in range(8)]
        mask = self.maskf if d == 0 else self.maskb
        for h in range(8):
            S.op("pool", MEMSET(C[:, h, :], 0.0), writes=[Cr[h]])
            S.op("pool", MEMSET(Cb[:, h, :], 0.0), writes=[Cbr[h]])
        for s in range(NSL):
            S.op("pool", MEMSET(VA[s][:, :, 256:257], 1.0), writes=[R["VA"][s]])
        if d == 1:
            HF = [self.sb(st, "HF_%d" % i, [128, D], F32) for i in range(2)]
            HFr = [Res() for _ in range(2)]
            OG = [self.sb(st, "OG_%d" % i, [128, D], F32) for i in range(2)]
            OGr = [Res() for _ in range(2)]
            SQ = self.sb(st, "SQ", [128, D], BF16)
            SQr = Res()
            HN = self.sb(st, "HN", [128, D], F32)
            HNr = Res()
            OB = [self.sb(st, "OB_%d" % i, [128, D], BF16) for i in range(2)]
            OBr = [Res() for _ in range(2)]
            OT = [self.sb(st, "OTt_%d" % i, [128, 16, 128], BF16) for i in range(2)]
            OTr = [Res() for _ in range(2)]
            rs = self.sb(st, "rs", [128, NT, 8, 3], F32)
            rsr = Res()
            S.op("pool", MEMSET(rs[:], 0.0), writes=[rsr])
            rml = Res()
            MLN, MLNr = self.load_bc(st, "MLN", I["ml_norm"][j:j + 1, :], rml)
        order = list(range(NT)) if d == 0 else [1, 0] + list(range(NT - 1, 1, -1))
        NCH = len(order)

        def pre(n):
            c = order[n]
            s = n % NSL
            rows = slice(c * 128, (c + 1) * 128)
            self.pace_casts(2)
            S.dma("sp", R["q32"][s], [DMA(q32[s][:], X["q_tm"][rows, :])], reads=[self.Xres["q_tm"]],
                  writes=[R["q32"][s]])
            S.dma("sp", R["k32"][s], [DMA(k32[s][:], X["k_tm"][rows, :])], reads=[self.Xres["k_tm"]],
                  writes=[R["k32"][s]])
            S.dma("sp", R["VA"][s], [DMA(VA[s][:, :, 0:256], X["v_tm"][rows, :].rearrange("p (h e) -> p h e", e=256))],
                  reads=[self.Xres["v_tm"]], writes=[R["VA"][s]])
            S.op("act", ACT(qb[s][:], q32[s][:], AF.Copy), reads=[R["q32"][s]], writes=[R["qb"][s]])
            S.op("dve", CP(kb[s][:], k32[s][:]), reads=[R["k32"][s]], writes=[R["kb"][s]])
            for h in range(8):
                S.op("act", ACT(kh[s][:, h, :], k32[s][:, h * 128:(h + 1) * 128], AF.Copy,
                                scale=self.WK[:, d, c, h:h + 1]), reads=[R["k32"][s], self.gres],
                     writes=[R["kh"][s]])
            self.transposes_to(qb[s], R["qb"][s], QT[s], R["QT"][s], nchunks=8, banks=(7,))
            self.transposes_to(kb[s], R["kb"][s], KT[s], R["KT"][s], nchunks=8, banks=(7,))
            for h in range(8):
                b = h // 4
                S.mm([MM(self.ps[b][:, (h % 4) * 128:(h % 4 + 1) * 128], KT[s][:, h, :], QT[s][:, h, :], True, True)],
                     reads=[R["KT"][s], R["QT"][s]], writes=[self.psres[b]])
            for h in range(8):
                b = h // 4
                S.op("dve", STT(PT[s][:, h, :], self.ps[b][:, (h % 4) * 128:(h % 4 + 1) * 128],
                                self.EK[:, d, c, h:h + 1], mask, ALU.mult, ALU.mult),
                     reads=[self.psres[b], self.gres, self.cres], writes=[PTr[s][h]])

        def heads(n):
            c = order[n]
            s = n % NSL
            rows = slice(c * 128, (c + 1) * 128)
            if d == 1:
                S.dma("sp", HFr[s], [DMA(HF[s][:], X["hf"][rows, :])], reads=[self.Xres["hf"]], writes=[HFr[s]])
                S.dma("sp", OGr[s], [DMA(OG[s][:], X["og_tm"][rows, :])], reads=[self.Xres["og_tm"]],
                      writes=[OGr[s]])
            for i in range(10):
                if i < 8:
                    h = i
                    tbk = 2 + (h % 3)
                    S.mm([MM(self.ps[tbk][:, 0:257], QT[s][:, h, :], Cb[:, h, 0:257], True, False),
                          MM(self.ps[tbk][:, 0:257], PT[s][:, h, :], VA[s][:, h, 0:257], False, True)],
                         reads=[R["QT"][s], Cbr[h], PTr[s][h], R["VA"][s]], writes=[self.psres[tbk]])
                    dbk = 5 + (h % 2)
                    S.mm([MM(self.ps[dbk][:, 0:257], kh[s][:, h, :], VA[s][:, h, 0:257], True, True)],
                         reads=[R["kh"][s], R["VA"][s]], writes=[self.psres[dbk]])
                if 1 <= i <= 8:
                    h = i - 1
                    tbk = 2 + (h % 3)
                    dbk = 5 + (h % 2)
                    S.op("dve", TT(dn[:, h, 2:3], self.ps[tbk][:, 256:257], self.EMB[:, d, c, h:h + 1], ALU.max),
                         reads=[self.psres[tbk], self.gres], writes=[dnr[h]])
                    S.op("dve", STT(dn[:, h, 0:1], self.ps[tbk][:, 256:257], -1.0, dn[:, h, 2:3], ALU.mult,
                                    ALU.max), reads=[self.psres[tbk], dnr[h]], writes=[dnr[h]])
                    S.op("dve", RCP(dn[:, h, 1:2], dn[:, h, 0:1]), reads=[dnr[h]], writes=[dnr[h]])
                    S.op("dve", STT(C[:, h, 0:257], C[:, h, 0:257], self.EBT[:, d, c, h:h + 1],
                                    self.ps[dbk][:, 0:257], ALU.mult, ALU.add),
                         reads=[Cr[h], self.gres, self.psres[dbk]], writes=[Cr[h]])
                if 2 <= i <= 9:
                    h = i - 2
                    tbk = 2 + (h % 3)
                    S.op("act", ACT(H[s][:, h * 256:(h + 1) * 256], self.ps[tbk][:, 0:256], AF.Copy,
                                    scale=dn[:, h, 1:2]), reads=[self.psres[tbk], dnr[h]], writes=[R["H"][s]])
                    S.op("act", ACT(Cb[:, h, 0:257], C[:, h, 0:257], AF.Copy), reads=[Cr[h]], writes=[Cbr[h]])
            if d == 0:
                S.dma("act", R["H"][s], [DMA(X["hf"][rows, :], H[s][:])], reads=[R["H"][s]],
                      writes=[self.Xres["hf"]])

        def readout(n):
            c = order[n]
            s = n % NSL
            S.op("dve", TT(H[s][:], H[s][:], HF[s][:], ALU.add), reads=[HFr[s]], writes=[R["H"][s]])
            for h in range(8):
                S.op("act", ACT(SQ[:, h * 256:(h + 1) * 256], H[s][:, h * 256:(h + 1) * 256], AF.Square,
                                accum_out=rs[:, c, h, 0:1]), reads=[R["H"][s]], writes=[SQr, rsr])
            S.op("act", ACT(rs[:, c, :, 1], rs[:, c, :, 0], AF.Sqrt, bias=EPS, scale=1.0 / 256.0), reads=[rsr],
                 writes=[rsr])
            S.op("dve", RCP(rs[:, c, :, 2], rs[:, c, :, 1]), reads=[rsr], writes=[rsr])
            S.op("dve", TT(HN[:], OG[s][:], MLN[:], ALU.mult), reads=[OGr[s], MLNr], writes=[HNr])
            for h in range(8):
                S.op("dve", STT(OB[s][:, h * 256:(h + 1) * 256], H[s][:, h * 256:(h + 1) * 256],
                                rs[:, c, h, 2:3], HN[:, h * 256:(h + 1) * 256], ALU.mult, ALU.mult),
                     reads=[R["H"][s], rsr, HNr], writes=[OBr[s]])
            self.transposes_to(OB[s], OBr[s], OT[s], OTr[s], banks=(7,))
            S.dma("act", OTr[s], [DMA(X["oT"][c], OT[s][:])], reads=[OTr[s]], writes=[self.Xres["oT"]])

        pre(0)
        pre(1)
        for n in range(NCH):
            heads(n)
            if d == 1 and n >= 1:
                readout(n - 1)
            if n + 2 < NCH:
                pre(n + 2)
        if d == 1:
            readout(NCH - 1)
        self.stage_end(st, "scan%d_%d" % (l, d))

    def layer_na(self, l, need_ctx):
        self.na_qkv(l, need_ctx)
        self.check("qkv%d" % l)
        self.na_att(l, need_ctx)

    def na_qkv(self, l, need_ctx):
        S, X = self.S, self.X
        j = l // 2
        st = self.stage_begin()
        self.ensure_casts("na_w_qkv", l)
        Wap, Wres = self.W["na_w_qkv"][j], self.Wres[("na_w_qkv", l)]
        g = self.gemm_setup(st)
        sbf = [self.sb(st, "qsb%d" % i, [128, 512], BF16) for i in range(4)]
        sbr = [Res() for _ in range(4)]
        ci = 0
        scale = 128.0 ** -0.5
        for (t0, nt) in self.blocks(True):
            ut = self.load_ut(g, X["uT"], self.Xres["uT"], t0, nt)
            n = nt * 128
            tok0 = t0 * 128
            for cb in range(8):
                if cb < 4 and t0 < 2 and not need_ctx:
                    continue

                def epi(o, ps, psr, cb=cb, n=n, tok0=tok0):
                    nonlocal ci
                    s = ci % 4
                    ci += 1
                    hd = (cb % 4) * 4 + o
                    if cb < 4:
                        S.op("act", ACT(sbf[s][:, 0:n], ps[:, 0:n], AF.Copy, scale=scale), reads=[psr],
                             writes=[sbr[s]])
                        S.dma("pool", sbr[s], [DMA(X["qT"][hd, :, tok0:tok0 + n], sbf[s][:, 0:n])], reads=[sbr[s]],
                              writes=[self.Xres["qT"]])
                    else:
                        S.op("dve", CP(sbf[s][:, 0:n], ps[:, 0:n]), reads=[psr], writes=[sbr[s]])
                        S.dma("pool", sbr[s], [DMA(X["kT"][hd, :, tok0:tok0 + n], sbf[s][:, 0:n])], reads=[sbr[s]],
                              writes=[self.Xres["kT"]])

                self.gemm_cols(g, "fm", [ut], nt, Wap, Wres, cb * 512, 512, epi)
            for cb in range(4):
                def epiv(o, ps, psr, cb=cb, t0=t0):
                    nonlocal ci
                    s = ci % 4
                    ci += 1
                    t = t0 + o
                    S.op("act" if (ci % 2) else "dve",
                         ACT(sbf[s][:], ps[:, :], AF.Copy) if (ci % 2) else CP(sbf[s][:], ps[:, :]), reads=[psr],
                         writes=[sbr[s]])
                    S.dma("pool", sbr[s], [DMA(X["v_tm"][t * 128:(t + 1) * 128, cb * 512:(cb + 1) * 512], sbf[s][:])],
                          reads=[sbr[s]], writes=[self.Xres["v_tm"]])

                self.gemm_cols(g, "tm", [ut], nt, Wap, Wres, 4096 + cb * 512, 512, epiv)
        self.stage_end(st, "qkv%d" % l)

    def na_att(self, l, need_ctx):
        S, X, I = self.S, self.X, self.I
        j = l // 2
        st = self.stage_begin()
        KT = [self.sb(st, "aKT%d" % i, [128, NTOK], BF16) for i in range(2)]
        QT = [self.sb(st, "aQT%d" % i, [128, NTOK], BF16) for i in range(2)]
        V = [self.sb(st, "aV%d" % i, [128, NT, 128], BF16) for i in range(2)]
        BI = [self.sb(st, "aBI%d" % i, [128, 21, 128], F32) for i in range(2)]
        OTh = [self.sb(st, "aOT%d" % i, [128, NT, 128], BF16) for i in range(2)]
        R = {n: [Res() for _ in range(2)] for n in ("KT", "QT", "V", "BI", "OT")}
        E = [self.sb(st, "aE%d" % i, [128, 5, 128], F32) for i in range(2)]
        Er = [Res() for _ in range(2)]
        PT = [self.sb(st, "aPT%d" % i, [128, 7, 128], BF16) for i in range(3)]
        PTr = [Res() for _ in range(3)]
        rd = [self.sb(st, "ard%d" % i, [128, 128], F32) for i in range(2)]
        rdr = [Res() for _ in range(2)]
        rbias = Res()
        qtiles = ([0, 1] if need_ctx else []) + list(range(2, NT))
        items = [(h, tq) for h in range(16) for tq in qtiles]

        def load_head(h):
            s = h % 2
            self.pace_casts(8)
            S.dma("sp", R["KT"][s], [DMA(KT[s][:], X["kT"][h])], reads=[self.Xres["kT"]], writes=[R["KT"][s]])
            S.dma("sp", R["QT"][s], [DMA(QT[s][:], X["qT"][h])], reads=[self.Xres["qT"]], writes=[R["QT"][s]])
            S.dma("sp", R["V"][s],
                  [DMA(V[s][:], X["v_tm"][:, h * 128:(h + 1) * 128].rearrange("(t p) e -> p t e", p=128))],
                  reads=[self.Xres["v_tm"]], writes=[R["V"][s]])
            S.dma("sp", R["BI"][s], [DMA(BI[s][:], I["na_bias"][j, h].rearrange("p (t q) -> p t q", q=128))],
                  reads=[rbias], writes=[R["BI"][s]])

        def cfg(tq):
            if tq < 2:
                return [], 0
            jq = tq - 2
            if 2 <= jq <= 29:
                return list(range(jq - 2, jq + 3)), 0
            if jq == 0:
                return [0, 1, 2, 3], 5
            if jq == 1:
                return [0, 1, 2, 3], 9
            if jq == 30:
                return [28, 29, 30, 31], 13
            return [28, 29, 30, 31], 17

        def phase_a(qi, h, tq):
            s = h % 2
            band, btile0 = cfg(tq)
            nb = len(band)
            x = qi % 2
            bA, bB = 2 * x, 2 * x + 1
            qs = slice(tq * 128, (tq + 1) * 128)
            chunks = [2 + m for m in band] + [0, 1]
            fA, fB = [], []
            for i, kt in enumerate(chunks):
                ks = slice(kt * 128, (kt + 1) * 128)
                if i < nb and i < 4:
                    fA.append(MM(self.ps[bA][:, i * 128:(i + 1) * 128], KT[s][:, ks], QT[s][:, qs], True, True))
                else:
                    slot = (i - 4) if i < nb else (1 + i - nb)
                    fB.append(MM(self.ps[bB][:, slot * 128:(slot + 1) * 128], KT[s][:, ks], QT[s][:, qs], True,
                                 True))
            if fA:
                S.mm(fA, reads=[R["KT"][s], R["QT"][s]], writes=[self.psres[bA]])
            S.mm(fB, reads=[R["KT"][s], R["QT"][s]], writes=[self.psres[bB]])

        def phase_b(qi, h, tq):
            s = h % 2
            band, btile0 = cfg(tq)
            nb = len(band)
            x = qi % 2
            p3 = qi % 3
            bA, bB, bO = 2 * x, 2 * x + 1, 4 + x
            chunks = [2 + m for m in band] + [0, 1]
            if nb:
                n4 = min(nb, 4)
                S.op("dve", TT(E[x][:, 0:n4, :], self.ps[bA][:, 0:n4 * 128].rearrange("p (a b) -> p a b", b=128),
                               BI[s][:, btile0:btile0 + n4, :], ALU.add), reads=[self.psres[bA], R["BI"][s]],
                     writes=[Er[x]])
                if nb == 5:
                    S.op("dve", TT(E[x][:, 4, :], self.ps[bB][:, 0:128], BI[s][:, btile0 + 4, :], ALU.add),
                         reads=[self.psres[bB], R["BI"][s]], writes=[Er[x]])
                S.op("act", ACT(PT[p3][:, 0:nb, :], E[x][:, 0:nb, :], AF.Exp), reads=[Er[x]], writes=[PTr[p3]])
            S.op("act", ACT(PT[p3][:, 5:7, :], self.ps[bB][:, 128:384].rearrange("p (a b) -> p a b", b=128),
                            AF.Exp), reads=[self.psres[bB]], writes=[PTr[p3]])

        def phase_c(qi, h, tq):
            s = h % 2
            band, btile0 = cfg(tq)
            nb = len(band)
            x = qi % 2
            p3 = qi % 3
            bO = 4 + x
            chunks = [2 + m for m in band] + [0, 1]
            fns = []
            nchunk = len(chunks)
            for i, kt in enumerate(chunks):
                pslot = i if i < nb else 5 + (i - nb)
                fns.append(MM(self.ps[bO][:, 0:128], V[s][:, kt, :], PT[p3][:, pslot, :], i == 0, i == nchunk - 1))
            for i, kt in enumerate(chunks):
                pslot = i if i < nb else 5 + (i - nb)
                fns.append(MM(self.ps[bO][:, 128:256], self.ones_bf, PT[p3][:, pslot, :], i == 0, i == nchunk - 1))
            S.mm(fns, reads=[R["V"][s], PTr[p3], self.cres], writes=[self.psres[bO]])
            S.op("dve", RCP(rd[x][:], self.ps[bO][:, 128:256]), reads=[self.psres[bO]], writes=[rdr[x]])
            S.op("dve", TT(OTh[s][:, tq, :], self.ps[bO][:, 0:128], rd[x][:], ALU.mult),
                 reads=[self.psres[bO], rdr[x]], writes=[R["OT"][s]])
            if tq == qtiles[-1]:
                t_lo = 0 if need_ctx else 2
                S.dma("act", R["OT"][s],
                      [DMA(X["oT"][t_lo:NT, :, h, :].rearrange("t p c -> p t c"), OTh[s][:, t_lo:NT, :])],
                      reads=[R["OT"][s]], writes=[self.Xres["oT"]])

        nit = len(items)
        for qi in range(nit + 2):
            if qi < nit:
                h, tq = items[qi]
                if tq == qtiles[0] and h == 0:
                    load_head(0)
                phase_a(qi, h, tq)
            if 0 <= qi - 1 < nit:
                phase_b(qi - 1, *items[qi - 1])
            if 0 <= qi - 2 < nit:
                phase_c(qi - 2, *items[qi - 2])
                ph, ptq = items[qi - 2]
                if ptq == qtiles[-1] and ph + 2 < 16:
                    load_head(ph + 2)
            if qi == 0 and 1 < 16:
                load_head(1)
        self.stage_end(st, "att%d" % l)


def _consts():
    cm = np.zeros((128, 512), np.float32)
    cm[:, 0:128] = np.eye(128, dtype=np.float32)
    cm[:, 128:256] = 1.0
    jj = np.arange(128)[:, None]
    ss = np.arange(128)[None, :]
    cm[:, 256:384] = (jj <= ss).astype(np.float32)
    cm[:, 384:512] = (jj >= ss).astype(np.float32)
    pos = np.arange(T)
    row = (pos // 64).astype(np.float32)
    col = (pos % 64).astype(np.float32)
    nf = 32
    inv = (np.float32(10000.0) ** (-np.arange(nf, dtype=np.float32) / np.float32(nf))).astype(np.float32)
    ang_r = row[:, None] * inv[None, :]
    ang_c = col[:, None] * inv[None, :]
    cos128 = np.concatenate([np.cos(ang_r), np.cos(ang_r), np.cos(ang_c), np.cos(ang_c)], axis=1)
    sin128 = np.concatenate([-np.sin(ang_r), np.sin(ang_r), -np.sin(ang_c), np.sin(ang_c)], axis=1)
    rope = np.stack([np.tile(cos128, (1, 4)), np.tile(sin128, (1, 4))], axis=1).astype(np.float32)
    return cm, rope


def _bias_index():
    cfgs = [(jq0, list(range(jq0 - 2, jq0 + 3))) for jq0 in [2]] + [(0, [0, 1, 2, 3]), (1, [0, 1, 2, 3]),
                                                                    (30, [28, 29, 30, 31]), (31, [28, 29, 30, 31])]
    ri = np.zeros((128, 21, 128), np.int64)
    cidx = np.zeros((128, 21, 128), np.int64)
    valid = np.zeros((128, 21, 128), bool)
    j = np.arange(128)[:, None]
    s = np.arange(128)[None, :]
    ti = 0
    for jq, band in cfgs:
        qr = 2 * jq + s // 64
        qc = s % 64
        rs = np.clip(qr - 4, 0, 56)
        cs = np.clip(qc - 8, 0, 48)
        for m in band:
            kr = 2 * m + j // 64
            kc = j % 64
            ok = (kr >= rs) & (kr < rs + 8) & (kc >= cs) & (kc < cs + 16)
            ro = np.clip(kr - qr + 7, 0, 14)
            co = np.clip(kc - qc + 15, 0, 30)
            ri[:, ti, :] = np.broadcast_to(ro, (128, 128))
            cidx[:, ti, :] = np.broadcast_to(co, (128, 128))
            valid[:, ti, :] = ok
            ti += 1
    assert ti == 21
    return ri, cidx, valid


_CACHE = {}


def _program():
    if "nc" not in _CACHE:
        p = Prog()
        _CACHE["nc"] = p.build()
    return _CACHE["nc"]


def _host_inputs(x, c, ctx, c_ctx, ada_w, ada_b, norm_mix, norm_mlp, ml_w_in, ml_b_gate, ml_norm, ml_w_out,
                 na_w_qkv, na_rpb, na_w_out, mlp_w1, mlp_w2, final_norm, ncores=8):
    f = lambda a: np.ascontiguousarray(np.asarray(a, dtype=np.float32))
    cm, rope = _consts()
    ri, cidx, valid = _bias_index()
    rpb = f(na_rpb)
    nb = rpb[:, :, ri, cidx]
    nb = np.where(valid[None, None], nb, np.float32(NEG)).astype(np.float32)
    nb = np.ascontiguousarray(nb.reshape(2, 16, 128, 21 * 128))
    shared = {
        "ada_w": f(ada_w), "ada_b": f(ada_b), "norm_mix": f(norm_mix), "norm_mlp": f(norm_mlp),
        "ml_w_in": f(ml_w_in), "ml_b_gate": f(ml_b_gate), "ml_norm": f(ml_norm), "ml_w_out": f(ml_w_out),
        "na_w_qkv": f(na_w_qkv), "na_bias": nb, "na_w_out": f(na_w_out), "mlp_w1": f(mlp_w1), "mlp_w2": f(mlp_w2),
        "final_norm": f(final_norm).reshape(1, D), "cmat": cm, "rope": rope,
    }
    cc = f(c_ctx).reshape(16, 128).T
    maps = []
    for b in range(ncores):
        cond = np.concatenate([f(c[b]).reshape(16, 128).T, cc], axis=1)
        m = dict(shared)
        m["x"] = f(x[b])
        m["ctx"] = f(ctx[b])
        m["cond"] = np.ascontiguousarray(cond)
        maps.append(m)
    return maps


def kernel(**inputs):
    nc = _program()
    maps = _host_inputs(**inputs)
    res = run_bass_kernel_spmd(nc, maps, core_ids=list(range(8)))
    return np.stack([np.asarray(r["y"], dtype=np.float32) for r in res.results], axis=0)
```
